# Optimizing a Trainium2 kernel written in Bass

```python
import jax, jax.numpy as jnp
from jax import lax
import numpy as np

D_MODEL = 2048
BATCH = 4
SEQ = 2048
DEPTH = 4

CHUNK = 64
A_HEAD_DIM = 128
A_WIDTH = D_MODEL // 2
A_HEADS = A_WIDTH // A_HEAD_DIM
A_LEFT_CHUNKS = 8
A_BAND = A_LEFT_CHUNKS + 1
A_MAX_REL = 256
B_WIDTH = D_MODEL - A_WIDTH
CONV_WIDTH = 3
C_BLOCK = 128
C_WIDTH = D_MODEL
C_GROUPS = 8
FFN_HIDDEN = -(-8 * D_MODEL // (3 * 256)) * 256
N_EVEN = (DEPTH + 1) // 2
N_ODD = DEPTH // 2
EPS = 1e-6
NEG_INF = -1e30

kernel_name = 'hybrid_chunk_attn_conv_gmlp_trunk'


def rms_norm(x, g):
    x32 = x.astype(jnp.float32)
    y = x32 * lax.rsqrt(jnp.mean(x32 * x32, axis=-1, keepdims=True) + EPS)
    return (y * g.astype(jnp.float32)).astype(x.dtype)


def layer_norm(x, g, b):
    x32 = x.astype(jnp.float32)
    mu = jnp.mean(x32, axis=-1, keepdims=True)
    xc = x32 - mu
    var = jnp.mean(xc * xc, axis=-1, keepdims=True)
    y = xc * lax.rsqrt(var + EPS) * g.astype(jnp.float32) + b.astype(jnp.float32)
    return y.astype(x.dtype)


def band_attention(q, k, v, rel_bias):
    bsz, seq, heads, dh = q.shape
    nc = seq // CHUNK
    qc = q.reshape(bsz, nc, CHUNK, heads, dh)
    pad = ((0, 0), (A_LEFT_CHUNKS * CHUNK, 0), (0, 0), (0, 0))
    kp = jnp.pad(k, pad).reshape(bsz, nc + A_LEFT_CHUNKS, CHUNK, heads, dh)
    vp = jnp.pad(v, pad).reshape(bsz, nc + A_LEFT_CHUNKS, CHUNK, heads, dh)
    band_idx = jnp.arange(nc)[:, None] + jnp.arange(A_BAND)[None, :]
    kb = kp[:, band_idx].reshape(bsz, nc, A_BAND * CHUNK, heads, dh)
    vb = vp[:, band_idx].reshape(bsz, nc, A_BAND * CHUNK, heads, dh)
    scores = jnp.einsum('bnqhd,bnkhd->bhnqk', qc, kb).astype(jnp.float32) * (dh ** -0.5)
    q_pos = jnp.arange(CHUNK)[:, None] + A_LEFT_CHUNKS * CHUNK
    k_pos = jnp.arange(A_BAND * CHUNK)[None, :]
    rel_idx = jnp.clip(q_pos - k_pos, -A_MAX_REL, A_MAX_REL) + A_MAX_REL
    bias = rel_bias.astype(jnp.float32)[:, rel_idx]
    scores = scores + bias[:, None]
    valid = jnp.repeat(band_idx >= A_LEFT_CHUNKS, CHUNK, axis=1)
    scores = jnp.where(valid[None, None, :, None, :], scores, NEG_INF)
    p = jax.nn.softmax(scores, axis=-1).astype(v.dtype)
    out = jnp.einsum('bhnqk,bnkhd->bnqhd', p, vb)
    return out.reshape(bsz, seq, heads * dh)


def gated_short_conv(b_gate, c_gate, h, conv_w):
    z = c_gate * h
    seq = z.shape[1]
    zp = jnp.pad(z, ((0, 0), (CONV_WIDTH - 1, 0), (0, 0)))
    y = conv_w[0] * zp[:, 0:seq]
    for j in range(1, CONV_WIDTH):
        y = y + conv_w[j] * zp[:, j:j + seq]
    return b_gate * y


def mixer_ab(h, w_in, rel_bias, conv_w, w_out):
    bsz, seq, _ = h.shape
    proj = h @ w_in
    cuts = [A_WIDTH, 2 * A_WIDTH, 3 * A_WIDTH, 3 * A_WIDTH + B_WIDTH, 3 * A_WIDTH + 2 * B_WIDTH]
    q, k, v, b_gate, c_gate, hv = jnp.split(proj, cuts, axis=-1)
    shp = (bsz, seq, A_HEADS, A_HEAD_DIM)
    attn = band_attention(q.reshape(shp), k.reshape(shp), v.reshape(shp), rel_bias)
    conv = gated_short_conv(b_gate, c_gate, hv, conv_w)
    return jnp.concatenate([attn, conv], axis=-1) @ w_out


def mixer_c(h, w_in, ln_g, ln_b, w_s, b_s, w_out):
    bsz, seq, _ = h.shape
    nb = seq // C_BLOCK
    z = jax.nn.gelu(h @ w_in, approximate=False)
    u, v = jnp.split(z, 2, axis=-1)
    v = layer_norm(v, ln_g, ln_b)
    pos = jnp.arange(C_BLOCK)
    mask = (pos[None, :] // CHUNK) <= (pos[:, None] // CHUNK)
    w_m = jnp.where(mask[None], w_s, jnp.zeros_like(w_s))
    vg = v.reshape(bsz, nb, C_BLOCK, C_GROUPS, C_WIDTH // C_GROUPS)
    s = jnp.einsum('gts,bnsgd->bntgd', w_m, vg) + jnp.transpose(b_s)[:, :, None]
    return (u * s.reshape(bsz, seq, C_WIDTH)) @ w_out


def swiglu(h, w_gate, w_up, w_down):
    return (jax.nn.silu(h @ w_gate) * (h @ w_up)) @ w_down


def setup_inputs(seed: int = 0) -> dict:
    key = jax.random.key(seed)
    ks = jax.random.split(key, 17)
    nrm = jax.random.normal
    f32 = jnp.float32
    in_ab = 3 * A_WIDTH + 3 * B_WIDTH
    return {
        'x': nrm(ks[0], (BATCH, SEQ, D_MODEL), f32),
        'mix_norm': 1.0 + 0.02 * nrm(ks[1], (DEPTH, D_MODEL), f32),
        'ab_w_in': nrm(ks[2], (N_EVEN, D_MODEL, in_ab), f32) * D_MODEL ** -0.5,
        'ab_rel_bias': 0.5 * nrm(ks[3], (N_EVEN, A_HEADS, 2 * A_MAX_REL + 1), f32),
        'ab_conv_w': nrm(ks[4], (N_EVEN, CONV_WIDTH, B_WIDTH), f32) * CONV_WIDTH ** -0.5,
        'ab_w_out': nrm(ks[5], (N_EVEN, A_WIDTH + B_WIDTH, D_MODEL), f32) * (A_WIDTH + B_WIDTH) ** -0.5,
        'c_w_in': nrm(ks[6], (N_ODD, D_MODEL, 2 * C_WIDTH), f32) * D_MODEL ** -0.5,
        'c_ln_g': 1.0 + 0.02 * nrm(ks[7], (N_ODD, C_WIDTH), f32),
        'c_ln_b': 0.02 * nrm(ks[8], (N_ODD, C_WIDTH), f32),
        'c_w_s': nrm(ks[9], (N_ODD, C_GROUPS, C_BLOCK, C_BLOCK), f32) * C_BLOCK ** -0.5,
        'c_b_s': 1.0 + 0.02 * nrm(ks[10], (N_ODD, C_GROUPS, C_BLOCK), f32),
        'c_w_out': nrm(ks[11], (N_ODD, C_WIDTH, D_MODEL), f32) * C_WIDTH ** -0.5,
        'ffn_norm': 1.0 + 0.02 * nrm(ks[12], (DEPTH, D_MODEL), f32),
        'ffn_w_gate': nrm(ks[13], (DEPTH, D_MODEL, FFN_HIDDEN), f32) * D_MODEL ** -0.5,
        'ffn_w_up': nrm(ks[14], (DEPTH, D_MODEL, FFN_HIDDEN), f32) * D_MODEL ** -0.5,
        'ffn_w_down': nrm(ks[15], (DEPTH, FFN_HIDDEN, D_MODEL), f32) * FFN_HIDDEN ** -0.5,
        'final_norm': 1.0 + 0.02 * nrm(ks[16], (D_MODEL,), f32),
    }


def reference(x, mix_norm, ab_w_in, ab_rel_bias, ab_conv_w, ab_w_out, c_w_in, c_ln_g, c_ln_b,
              c_w_s, c_b_s, c_w_out, ffn_norm, ffn_w_gate, ffn_w_up, ffn_w_down, final_norm):
    for layer in range(DEPTH):
        i = layer // 2
        h = rms_norm(x, mix_norm[layer])
        if layer % 2 == 0:
            x = x + mixer_ab(h, ab_w_in[i], ab_rel_bias[i], ab_conv_w[i], ab_w_out[i])
        else:
            x = x + mixer_c(h, c_w_in[i], c_ln_g[i], c_ln_b[i], c_w_s[i], c_b_s[i], c_w_out[i])
        h = rms_norm(x, ffn_norm[layer])
        x = x + swiglu(h, ffn_w_gate[layer], ffn_w_up[layer], ffn_w_down[layer])
    return rms_norm(x, final_norm)
```

```python
import contextlib
import numpy as np
import concourse.bass as bass
import concourse.mybir as mybir
from concourse.bass_utils import run_bass_kernel_spmd

F32 = mybir.dt.float32
BF16 = mybir.dt.bfloat16
AF = mybir.ActivationFunctionType
ALU = mybir.AluOpType
AX = mybir.AxisListType

P = 128
D = 2048
NCH = 16
T = 1024
HALO = 512
FF = 5632
NHC = 44
DEPTH = 4
EPS = 1e-6
NSLOT = 4
SLAB = 4096
NEG = -30000.0
SCALE = 128 ** -0.5
N_CORES = 8
import os as _os
NROT = int(_os.environ.get('K_NROT', '6'))


class Buf:
    __slots__ = ("w", "r")

    def __init__(self):
        self.w = {}
        self.r = {}


class Sched:
    ENGS = ("pe", "act", "dve", "pool", "sp")

    def __init__(self):
        self.q = {e: [] for e in self.ENGS}
        self.cnt = {}
        self.seen = {e: {} for e in self.ENGS}
        self.keys = set(self.ENGS)

    def _deps(self, eng, reads, writes, extra):
        need = {}

        def add(d):
            for k, v in d.items():
                if v > need.get(k, 0):
                    need[k] = v

        for b in reads:
            add(b.w)
        for b in writes:
            add(b.w)
            add(b.r)
        for t in extra:
            if t is not None:
                add({t[0]: t[1]})
        waits = []
        seen = self.seen[eng]
        for k, v in need.items():
            if seen.get(k, 0) < v:
                waits.append((k, v))
                seen[k] = v
        return waits

    def raw(self, queue, key, inc, fn, reads=(), writes=(), extra=()):
        waits = self._deps(queue, reads, writes, extra)
        self.keys.add(key)
        self.cnt[key] = self.cnt.get(key, 0) + inc
        tok = (key, self.cnt[key])
        self.q[queue].append((waits, fn, key, inc))
        for b in reads:
            if tok[1] > b.r.get(key, 0):
                b.r[key] = tok[1]
        for b in writes:
            b.w = {key: tok[1]}
            b.r = {}
        return tok

    def op(self, eng, fn, reads=(), writes=(), extra=()):
        return self.raw(eng, eng, 1, fn, reads, writes, extra)

    def dma(self, queue, key, fn, reads=(), writes=(), extra=()):
        return self.raw(queue, key, 16, fn, reads, writes, extra)

    def wait_only(self, eng, toks):
        waits = self._deps(eng, (), (), toks)
        self.q[eng].append((waits, None, None, 0))

    def replay(self, eng, e, sems):
        for waits, fn, key, inc in self.q[eng]:
            for (k, v) in waits:
                e.wait_ge(sems[k], v)
            if fn is not None:
                fn(e).then_inc(sems[key], inc)


class Arena:
    def __init__(self, ap, nbytes):
        self.ap = ap
        self.n = nbytes
        self.off = 0

    def mark(self):
        return self.off

    def release(self, m):
        self.off = m

    def alloc(self, nbytes, dtype, shape=None):
        nbytes = (nbytes + 31) // 32 * 32
        o = self.off
        if o + nbytes > self.n:
            raise RuntimeError(f"arena overflow: {o}+{nbytes} > {self.n}")
        self.off = o + nbytes
        v = self.ap[:, o // 2:(o + nbytes) // 2]
        if dtype == F32:
            v = v.bitcast(F32)
        if shape is not None:
            if len(shape) == 2:
                v = v.rearrange("p (a b) -> p a b", b=shape[1])
            elif len(shape) == 3:
                v = v.rearrange("p (a b c) -> p a b c", b=shape[1], c=shape[2])
        return v


class Builder:
    def __init__(self, layers, first, last):
        self.layers = list(layers)
        self.first = first
        self.last = last
        self.S = Sched()
        self.nc = bass.Bass("TRN2", target_bir_lowering=False)
        self.slab_i = 0
        self.ps_i = 0
        self.barrier = None

    def declare(self):
        nc = self.nc
        L = self.layers
        ne = len([l for l in L if l % 2 == 0])
        no = len([l for l in L if l % 2 == 1])
        nl = len(L)
        self.ne, self.no, self.nl = ne, no, nl

        def din(name, shape, dt=F32):
            return nc.dram_tensor(name, list(shape), dt, kind="ExternalInput").ap()

        self.d_x = din("xT_in", [D, T])
        self.d_out = nc.dram_tensor("yT_out", [D, T], F32, kind="ExternalOutput").ap()
        self.d_gmix = din("g_mix", [P, nl * NCH])
        self.d_gffn = din("g_ffn", [P, nl * NCH])
        self.d_gfin = din("g_fin", [P, NCH])
        self.d_flag = din("flags", [P, 2])
        import os
        self.dbg = os.environ.get("K_DBG", "")
        if "noffn" not in self.dbg:
            self.d_wg = din("w_gate", [nl, D, FF])
            self.d_wu = din("w_up", [nl, D, FF])
            self.d_wd = din("w_down", [nl, FF, D])
        if ne:
            self.d_win = din("ab_w_in", [ne, D, 6144])
            self.d_wout = din("ab_w_out", [ne, D, D])
            self.d_bias = din("ab_biasg", [ne * 4, P, 2 * 640])
            self.d_mask = din("ab_maskneg", [P, 640])
            self.d_cw = din("ab_convw", [P, ne * 24])
            self.d_xh = din("xh_in", [ne, D, HALO])
        if no:
            self.d_cwin = din("c_w_in", [no, D, 4096])
            self.d_cwout = din("c_w_out", [no, D, D])
            self.d_lng = din("c_ln_g_rep", [no, P, D])
            self.d_lnb = din("c_ln_b_rep", [no, P, D])
            self.d_wsT = din("c_w_sT", [no, 8, P, P])
            self.d_bs = din("c_b_s", [no, 1, 1024])

    def psalloc(self):
        i = self.ps_i % NROT
        self.ps_i += 1
        return self.ps[i], self.psb[i]

    def slab(self, src_ap):
        s = self.slab_i % NSLOT
        self.slab_i += 1
        slot = self.ring[s]
        buf = self.ringb[s]
        return s, slot, buf, src_ap

    def slab_in(self, w2d, c0):
        s = self.slab_i % NSLOT
        self.slab_i += 1
        view = self.ring[s][:, :].rearrange("p (c n) -> p c n", n=256)
        src = w2d.rearrange("(c p) n -> p c n", p=P)[:, :, c0:c0 + 256]
        self.S.dma("pool", f"slot{s}", lambda e, o=view, i=src: e.dma_start(out=o, in_=i),
                   writes=[self.ringb[s]])
        return view, self.ringb[s]

    def slab_rows(self, w2d, r0):
        s = self.slab_i % NSLOT
        self.slab_i += 1
        view = self.ring[s][:, :].rearrange("p (k n) -> p k n", n=2048)
        src = w2d[r0:r0 + 256, :].rearrange("(k p) n -> p k n", p=P)
        self.S.dma("pool", f"slot{s}", lambda e, o=view, i=src: e.dma_start(out=o, in_=i),
                   writes=[self.ringb[s]])
        return view, self.ringb[s]

    def mm(self, out, lhsT, rhs, start, stop, reads, writes):
        return self.S.op("pe", lambda e: e.matmul(out, lhsT=lhsT, rhs=rhs, start=start, stop=stop),
                         reads=reads, writes=writes)

    def xadd(self, dc, half, pb, pbb):
        xs = self.xT[:, dc, half * 512:(half + 1) * 512]
        return self.S.op("dve", lambda e: e.tensor_tensor(out=xs, in0=xs, in1=pb[:, :], op=ALU.add),
                         reads=[pbb], writes=[self.xb[dc][half]])

    def proj_out(self, Wv, Wb, nk, acts, actb):
        tok = None
        for dc in range(NCH):
            for half in range(2):
                pb, pbb = self.psalloc()
                for k in range(nk):
                    self.mm(pb[:, :], Wv[:, k, dc * 128:(dc + 1) * 128],
                            acts[k][:, half * 512:(half + 1) * 512], k == 0, k == nk - 1,
                            reads=[Wb, actb[k]], writes=[pbb])
                tok = self.xadd(dc, half, pb, pbb)
        return tok

    def rmsnorm(self, gain, final=False):
        S = self.S
        pa = [self.psalloc(), self.psalloc()]
        for c in range(NCH):
            sq = self.sq[:, c % 2, :]
            sqb = self.sqb[c % 2]
            S.op("act", lambda e, o=sq, i=self.xT[:, c, :]: e.activation(out=o, in_=i, func=AF.Square),
                 reads=self.xb[c], writes=[sqb])
            for half in range(2):
                self.mm(pa[half][0][:, :], self.ones[:, :], sq[:, half * 512:(half + 1) * 512],
                        c == 0, c == NCH - 1, reads=[sqb, self.constb], writes=[pa[half][1]])
        for half in range(2):
            rs = self.rs[:, half * 512:(half + 1) * 512]
            S.op("act", lambda e, o=rs, i=pa[half][0][:, :]: e.activation(
                out=o, in_=i, func=AF.Sqrt, bias=self.epsc[:, 0:1], scale=1.0 / D),
                reads=[pa[half][1], self.constb], writes=[self.rsb[half]])
            S.op("dve", lambda e, o=rs: e.reciprocal(out=o, in_=o),
                 reads=[self.rsb[half]], writes=[self.rsb[half]])
        tok = None
        outs = []
        for c in range(NCH):
            if not final:
                o = self.hT[:, c, :]
                tok = S.op("dve", lambda e, o=o, i=self.xT[:, c, :], g=gain[:, c:c + 1]:
                           e.scalar_tensor_tensor(out=o, in0=i, scalar=g, in1=self.rs[:, :],
                                                  op0=ALU.mult, op1=ALU.mult),
                           reads=self.xb[c] + self.rsb + [self.constb], writes=[self.hb[c]])
            else:
                ob = self.obuf[:, c % 2, :]
                S.op("dve", lambda e, o=ob, i=self.xT[:, c, :], g=gain[:, c:c + 1]:
                     e.scalar_tensor_tensor(out=o, in0=i, scalar=g, in1=self.rs[:, :],
                                            op0=ALU.mult, op1=ALU.mult),
                     reads=self.xb[c] + self.rsb + [self.constb], writes=[self.obb[c % 2]])
                dst = self.d_out[c * 128:(c + 1) * 128, :]
                outs.append(S.dma("sp", f"out{c % 2}", lambda e, o=dst, i=ob: e.dma_start(out=o, in_=i),
                                  reads=[self.obb[c % 2]]))
        self.barrier = tok
        return outs

    def ffn(self, li):
        S = self.S
        A = self.arena
        m = A.mark()
        act = A.alloc(4 * T * 2, BF16, (4, T))
        actb = [Buf() for _ in range(4)]
        sg = A.alloc(2 * 512 * 4, F32, (2, 512))
        sgb = [Buf(), Buf()]
        wg, wu, wd = self.d_wg[li], self.d_wu[li], self.d_wd[li]
        for grp in range(NHC // 4):
            for jj in range(2):
                j = grp * 2 + jj
                Gv, Gb = self.slab_in(wg, j * 256)
                Uv, Ub = self.slab_in(wu, j * 256)
                for i in range(2):
                    hc = jj * 2 + i
                    pg = [self.psalloc(), self.psalloc()]
                    for kc in range(NCH):
                        for half in range(2):
                            self.mm(pg[half][0][:, :], Gv[:, kc, i * 128:(i + 1) * 128],
                                    self.hT[:, kc, half * 512:(half + 1) * 512], kc == 0, kc == NCH - 1,
                                    reads=[Gb, self.hb[kc]], writes=[pg[half][1]])
                    pu = [self.psalloc(), self.psalloc()]
                    for kc in range(NCH):
                        for half in range(2):
                            self.mm(pu[half][0][:, :], Uv[:, kc, i * 128:(i + 1) * 128],
                                    self.hT[:, kc, half * 512:(half + 1) * 512], kc == 0, kc == NCH - 1,
                                    reads=[Ub, self.hb[kc]], writes=[pu[half][1]])
                    for half in range(2):
                        S.op("act", lambda e, o=sg[:, half, :], i=pg[half][0][:, :]:
                             e.activation(out=o, in_=i, func=AF.Silu),
                             reads=[pg[half][1]], writes=[sgb[half]], extra=[self.barrier])
                        S.op("dve", lambda e, o=act[:, hc, half * 512:(half + 1) * 512], a=sg[:, half, :],
                             b=pu[half][0][:, :]: e.tensor_tensor(out=o, in0=a, in1=b, op=ALU.mult),
                             reads=[sgb[half], pu[half][1]], writes=[actb[hc]])
            D0v, D0b = self.slab_rows(wd, (grp * 4) * 128)
            D1v, D1b = self.slab_rows(wd, (grp * 4 + 2) * 128)
            for dc in range(NCH):
                for half in range(2):
                    pb, pbb = self.psalloc()
                    for k in range(4):
                        Dv, Db = (D0v, D0b) if k < 2 else (D1v, D1b)
                        self.mm(pb[:, :], Dv[:, k % 2, dc * 128:(dc + 1) * 128],
                                act[:, k, half * 512:(half + 1) * 512], k == 0, k == 3,
                                reads=[Db, actb[k]], writes=[pbb])
                    self.xadd(dc, half, pb, pbb)
        A.release(m)

    def mixer_even(self, ei, gain):
        S = self.S
        nc = self.nc
        A = self.arena
        m = A.mark()
        bar = [self.barrier]
        win = self.d_win[ei]
        wout = self.d_wout[ei]
        hh_t = A.alloc(NCH * HALO * 2, BF16, (NCH, HALO))
        hhb = Buf()
        stg = A.alloc(2 * HALO * 4, F32, (2, HALO))
        stgb = [Buf(), Buf()]
        sqh = A.alloc(2 * HALO * 2, BF16, (2, HALO))
        sqhb = [Buf(), Buf()]
        rsh = A.alloc(HALO * 4, F32)
        rshb = Buf()
        xh = self.d_xh[ei].rearrange("(c p) t -> p c t", p=P)
        pa = self.psalloc()
        for c in range(NCH):
            s_ = c % 2
            S.dma("sp", f"xh{s_}", lambda e, o=stg[:, s_, :], i=xh[:, c, :]: e.dma_start(out=o, in_=i),
                  writes=[stgb[s_]], extra=bar)
            S.op("act", lambda e, o=sqh[:, s_, :], i=stg[:, s_, :]: e.activation(out=o, in_=i, func=AF.Square),
                 reads=[stgb[s_]], writes=[sqhb[s_]], extra=bar)
            self.mm(pa[0][:, :], self.ones[:, :], sqh[:, s_, :], c == 0, c == NCH - 1,
                    reads=[sqhb[s_], self.constb], writes=[pa[1]])
        S.op("act", lambda e: e.activation(out=rsh, in_=pa[0][:, :], func=AF.Sqrt, bias=self.epsc[:, 0:1],
                                           scale=1.0 / D), reads=[pa[1], self.constb], writes=[rshb], extra=bar)
        S.op("dve", lambda e: e.reciprocal(out=rsh, in_=rsh), reads=[rshb], writes=[rshb])
        for c in range(NCH):
            s_ = c % 2
            S.dma("sp", f"xh{s_}", lambda e, o=stg[:, s_, :], i=xh[:, c, :]: e.dma_start(out=o, in_=i),
                  writes=[stgb[s_]], extra=bar)
            S.op("dve", lambda e, o=hh_t[:, c, :], i=stg[:, s_, :], g=gain[:, c:c + 1]:
                 e.scalar_tensor_tensor(out=o, in0=i, scalar=g, in1=rsh, op0=ALU.mult, op1=ALU.mult),
                 reads=[stgb[s_], rshb, self.constb], writes=[hhb], extra=bar)
        mask = A.alloc(640 * 4, F32)
        maskb = Buf()
        S.dma("sp", "cm", lambda e: e.dma_start(out=mask, in_=self.d_mask[:, :]), writes=[maskb], extra=bar)
        m2 = A.mark()
        qT = A.alloc(2 * T * 2, BF16, (2, T))
        kT = A.alloc(2 * (T + HALO) * 2, BF16, (2, T + HALO))
        Vt = A.alloc(12 * 256 * 2, BF16, (12, 256))
        tb = A.alloc(2 * 640 * 4, F32, (2, 640))
        tmp = A.alloc(2 * 640 * 4, F32, (2, 640))
        Pb = A.alloc(6 * 640 * 2, BF16, (6, 640))
        rinv = A.alloc(512 * 4, F32)
        ao = A.alloc(2 * T * 2, BF16, (2, T))
        qb = [Buf(), Buf()]
        kb = [Buf(), Buf()]
        Vb = [Buf() for _ in range(12)]
        tbb = Buf()
        tmpb = [Buf(), Buf()]
        Pbb = [Buf() for _ in range(6)]
        rinvb = Buf()
        aob = [Buf(), Buf()]
        ntmp = 0
        last = None
        for g in range(0 if "noattn" not in self.dbg else 4, 4):
            Qv, Qb = self.slab_in(win, g * 256)
            Kv, Kb = self.slab_in(win, 1024 + g * 256)
            Vv, Vvb = self.slab_in(win, 2048 + g * 256)
            S.dma("sp", "tb", lambda e, g=g: e.dma_start(out=tb, in_=self.d_bias[ei * 4 + g].rearrange(
                "p (a b) -> p a b", b=640)), writes=[tbb], extra=bar + [last])
            for hh in range(2):
                S.op("dve", lambda e, o=tb[:, hh, :]: e.tensor_tensor(out=o, in0=o, in1=mask, op=ALU.add),
                     reads=[maskb, tbb], writes=[tbb])
            for hh in range(2):
                for half in range(2):
                    pb, pbb = self.psalloc()
                    for kc in range(NCH):
                        self.mm(pb[:, :], Qv[:, kc, hh * 128:(hh + 1) * 128],
                                self.hT[:, kc, half * 512:(half + 1) * 512], kc == 0, kc == NCH - 1,
                                reads=[Qb, self.hb[kc]], writes=[pbb])
                    S.op("act", lambda e, o=qT[:, hh, half * 512:(half + 1) * 512], i=pb[:, :]:
                         e.activation(out=o, in_=i, func=AF.Copy),
                         reads=[pbb], writes=[qb[hh]], extra=bar + [last])
            for hh in range(2):
                for seg in (1, 2, 0):
                    pb, pbb = self.psalloc()
                    for kc in range(NCH):
                        if seg == 0:
                            rhs, rb = hh_t[:, kc, :], hhb
                        else:
                            rhs, rb = self.hT[:, kc, (seg - 1) * 512:seg * 512], self.hb[kc]
                        self.mm(pb[:, :], Kv[:, kc, hh * 128:(hh + 1) * 128], rhs, kc == 0, kc == NCH - 1,
                                reads=[Kb, rb], writes=[pbb])
                    S.op("act", lambda e, o=kT[:, hh, seg * 512:(seg + 1) * 512], i=pb[:, :]:
                         e.activation(out=o, in_=i, func=AF.Copy),
                         reads=[pbb], writes=[kb[hh]], extra=bar + [last])
            for tp in (2, 3, 4, 5, 0, 1):
                pb, pbb = self.psalloc()
                for t2 in range(2):
                    tt = tp * 2 + t2
                    for kc in range(NCH):
                        if tt < 4:
                            lh, lb = hh_t[:, kc, tt * 128:(tt + 1) * 128], hhb
                        else:
                            lh, lb = self.hT[:, kc, (tt - 4) * 128:(tt - 3) * 128], self.hb[kc]
                        self.mm(pb[:, t2 * 256:(t2 + 1) * 256], lh, Vv[:, kc, :], kc == 0, kc == NCH - 1,
                                reads=[Vvb, lb], writes=[pbb])
                S.op("dve", lambda e, o=Vt[:, tp * 2:tp * 2 + 2, :], i=pb[:, :].rearrange(
                    "p (a b) -> p a b", b=256): e.tensor_copy(out=o, in_=i),
                    reads=[pbb], writes=[Vb[tp * 2], Vb[tp * 2 + 1]], extra=bar + [last])
            for hh in range(2):
                pv = rsb_ = None
                for j in range(12 if "noscore" not in self.dbg else 0):
                    qc0 = max(0, 2 * j - 8)
                    qc1 = min(16, 2 * j + 2)
                    nq = (qc1 - qc0) * 64
                    q0 = qc0 * 64
                    c0 = (qc0 + 8 - 2 * j) * 64
                    tsel = ntmp % 2
                    ntmp += 1
                    for (a0, a1) in ((0, min(nq, 512)), (512, nq)):
                        if a1 <= a0:
                            continue
                        pb, pbb = self.psalloc()
                        self.mm(pb[:, 0:a1 - a0], kT[:, hh, j * 128:(j + 1) * 128],
                                qT[:, hh, q0 + a0:q0 + a1], True, True,
                                reads=[kb[hh], qb[hh]], writes=[pbb])
                        S.op("dve", lambda e, o=tmp[:, tsel, a0:a1], i=pb[:, 0:a1 - a0],
                             t=tb[:, hh, c0 + a0:c0 + a1]: e.scalar_tensor_tensor(
                                 out=o, in0=i, scalar=SCALE, in1=t, op0=ALU.mult, op1=ALU.add),
                             reads=[pbb, tbb], writes=[tmpb[tsel]])
                    bias_ap = self.flags[:, 1:2] if j < 4 else self.zeroc[:, 0:1]
                    S.op("act", lambda e, o=Pb[:, j % 6, 0:nq], i=tmp[:, tsel, 0:nq], b=bias_ap:
                         e.activation(out=o, in_=i, func=AF.Exp, bias=b),
                         reads=[tmpb[tsel], self.constb], writes=[Pbb[j % 6]])
                    if j >= 4 and "nopv" not in self.dbg:
                        p = j - 4
                        slot = p % 4
                        if slot == 0:
                            pv = (self.ps[NROT], self.psb[NROT])
                            rsb_ = (self.ps[NROT + 1], self.psb[NROT + 1])
                        for jj in range(p, p + 5):
                            col = (2 * p - max(0, 2 * jj - 8)) * 64
                            rhs = Pb[:, jj % 6, col:col + 128]
                            self.mm(pv[0][:, slot * 128:(slot + 1) * 128], Vt[:, jj, hh * 128:(hh + 1) * 128],
                                    rhs, jj == p, jj == p + 4, reads=[Vb[jj], Pbb[jj % 6]], writes=[pv[1]])
                            self.mm(rsb_[0][:, slot * 128:(slot + 1) * 128], self.ones[:, :],
                                    rhs, jj == p, jj == p + 4, reads=[self.constb, Pbb[jj % 6]],
                                    writes=[rsb_[1]])
                        if slot == 3:
                            rnd = p // 4
                            S.op("dve", lambda e, i=rsb_[0][:, :]: e.reciprocal(out=rinv, in_=i),
                                 reads=[rsb_[1]], writes=[rinvb])
                            S.op("dve", lambda e, o=ao[:, hh, rnd * 512:(rnd + 1) * 512], i=pv[0][:, :]:
                                 e.tensor_tensor(out=o, in0=i, in1=rinv, op=ALU.mult),
                                 reads=[pv[1], rinvb], writes=[aob[hh]], extra=bar + [last])
            Ov, Ob = self.slab_rows(wout, g * 256)
            last = self.proj_out(Ov, Ob, 2, [ao[:, 0, :], ao[:, 1, :]], aob)
        A.release(m2)
        csb = A.alloc((T + 2) * 4, F32)
        z = A.alloc((T + 2) * 4, F32)
        y = A.alloc(T * 4, F32)
        co = A.alloc(2 * T * 2, BF16, (2, T))
        csbb, zb, yb = Buf(), Buf(), Buf()
        cob = [Buf(), Buf()]
        cbar = bar + [last]
        for cc in range(0 if "noconv" not in self.dbg else 4, 4):
            Bv, Bb = self.slab_in(win, 3072 + cc * 256)
            Cv, Cb = self.slab_in(win, 4096 + cc * 256)
            Hv, Hb = self.slab_in(win, 5120 + cc * 256)
            for ii in range(2):
                ch = cc * 2 + ii
                cw = self.convw[:, ei * 24:(ei + 1) * 24]
                pc = [self.psalloc(), self.psalloc()]
                for kc in range(NCH):
                    for half in range(2):
                        self.mm(pc[half][0][:, :], Cv[:, kc, ii * 128:(ii + 1) * 128],
                                self.hT[:, kc, half * 512:(half + 1) * 512], kc == 0, kc == NCH - 1,
                                reads=[Cb, self.hb[kc]], writes=[pc[half][1]])
                ph_ = self.psalloc()
                for kc in range(NCH):
                    self.mm(ph_[0][:, 0:2], Cv[:, kc, ii * 128:(ii + 1) * 128], hh_t[:, kc, HALO - 2:HALO],
                            kc == 0, kc == NCH - 1, reads=[Cb, hhb], writes=[ph_[1]])
                for kc in range(NCH):
                    self.mm(ph_[0][:, 2:4], Hv[:, kc, ii * 128:(ii + 1) * 128], hh_t[:, kc, HALO - 2:HALO],
                            kc == 0, kc == NCH - 1, reads=[Hb, hhb], writes=[ph_[1]])
                for half in range(2):
                    S.op("act", lambda e, o=csb[:, 2 + half * 512:2 + (half + 1) * 512], i=pc[half][0][:, :]:
                         e.activation(out=o, in_=i, func=AF.Copy),
                         reads=[pc[half][1]], writes=[csbb], extra=cbar)
                S.op("act", lambda e, o=csb[:, 0:2], i=ph_[0][:, 0:2]: e.activation(out=o, in_=i, func=AF.Copy),
                     reads=[ph_[1]], writes=[csbb], extra=cbar)
                ph = [self.psalloc(), self.psalloc()]
                for kc in range(NCH):
                    for half in range(2):
                        self.mm(ph[half][0][:, :], Hv[:, kc, ii * 128:(ii + 1) * 128],
                                self.hT[:, kc, half * 512:(half + 1) * 512], kc == 0, kc == NCH - 1,
                                reads=[Hb, self.hb[kc]], writes=[ph[half][1]])
                for half in range(2):
                    S.op("dve", lambda e, o=z[:, 2 + half * 512:2 + (half + 1) * 512],
                         a=csb[:, 2 + half * 512:2 + (half + 1) * 512], b=ph[half][0][:, :]:
                         e.tensor_tensor(out=o, in0=a, in1=b, op=ALU.mult),
                         reads=[csbb, ph[half][1]], writes=[zb], extra=cbar)
                S.op("dve", lambda e, o=z[:, 0:2], a=csb[:, 0:2], b=ph_[0][:, 2:4]:
                     e.scalar_tensor_tensor(out=o, in0=a, scalar=self.flags[:, 0:1], in1=b,
                                            op0=ALU.mult, op1=ALU.mult),
                     reads=[csbb, ph_[1], self.constb], writes=[zb], extra=cbar)
                pbq = [self.psalloc(), self.psalloc()]
                for kc in range(NCH):
                    for half in range(2):
                        self.mm(pbq[half][0][:, :], Bv[:, kc, ii * 128:(ii + 1) * 128],
                                self.hT[:, kc, half * 512:(half + 1) * 512], kc == 0, kc == NCH - 1,
                                reads=[Bb, self.hb[kc]], writes=[pbq[half][1]])
                w0 = cw[:, 0 * 8 + ch:0 * 8 + ch + 1]
                w1 = cw[:, 1 * 8 + ch:1 * 8 + ch + 1]
                w2 = cw[:, 2 * 8 + ch:2 * 8 + ch + 1]
                S.op("dve", lambda e, w=w0: e.tensor_scalar(out=y, in0=z[:, 0:T], scalar1=w, scalar2=None,
                                                            op0=ALU.mult),
                     reads=[zb, self.constb], writes=[yb], extra=cbar)
                S.op("dve", lambda e, w=w1: e.scalar_tensor_tensor(out=y, in0=z[:, 1:T + 1], scalar=w, in1=y,
                                                                   op0=ALU.mult, op1=ALU.add),
                     reads=[zb, yb, self.constb], writes=[yb])
                S.op("dve", lambda e, w=w2: e.scalar_tensor_tensor(out=y, in0=z[:, 2:T + 2], scalar=w, in1=y,
                                                                   op0=ALU.mult, op1=ALU.add),
                     reads=[zb, yb, self.constb], writes=[yb])
                for half in range(2):
                    S.op("dve", lambda e, o=co[:, ii, half * 512:(half + 1) * 512],
                         a=y[:, half * 512:(half + 1) * 512], b=pbq[half][0][:, :]:
                         e.tensor_tensor(out=o, in0=a, in1=b, op=ALU.mult),
                         reads=[yb, pbq[half][1]], writes=[cob[ii]], extra=cbar)
            Ov, Ob = self.slab_rows(wout, 1024 + cc * 256)
            self.proj_out(Ov, Ob, 2, [co[:, 0, :], co[:, 1, :]], cob)
        A.release(m)

    def mixer_odd(self, oi):
        S = self.S
        A = self.arena
        m = A.mark()
        bar = [self.barrier]
        win = self.d_cwin[oi]
        wout = self.d_cwout[oi]
        vt = A.alloc(8 * D * 2, BF16, (8, D))
        vtb = [Buf() for _ in range(8)]
        grep_ = A.alloc(D * 4, F32)
        brep = A.alloc(D * 4, F32)
        gbb = Buf()
        S.dma("sp", "lg", lambda e: e.dma_start(out=grep_, in_=self.d_lng[oi]), writes=[gbb], extra=bar)
        S.dma("sp", "lb", lambda e: e.dma_start(out=brep, in_=self.d_lnb[oi]), writes=[gbb], extra=bar)
        wmT = A.alloc(8 * P * 2, BF16, (8, P))
        wmb = Buf()
        S.dma("pool", "wm", lambda e: e.dma_start(out=wmT, in_=self.d_wsT[oi].rearrange("g s t -> s g t")),
              writes=[wmb], extra=bar)
        S.op("dve", lambda e: e.memset(wmT[64:128, :, 0:64], 0.0), writes=[wmb])
        tmp = A.alloc(T * 4, F32)
        tmpb = Buf()
        bsf = tmp
        bsh = A.alloc(1024 * 2, BF16)
        bsl = A.alloc(1024 * 2, BF16)
        bsb = Buf()
        S.dma("sp", "bs", lambda e: e.dma_start(out=bsf[0:1, :], in_=self.d_bs[oi]), writes=[bsb, tmpb], extra=bar)
        S.op("dve", lambda e: e.tensor_copy(out=bsh[0:1, :], in_=bsf[0:1, :]), reads=[bsb, tmpb], writes=[bsb])
        S.op("dve", lambda e: e.tensor_tensor(out=bsf[0:1, :], in0=bsf[0:1, :], in1=bsh[0:1, :], op=ALU.subtract),
             reads=[bsb], writes=[bsb, tmpb])
        S.op("dve", lambda e: e.tensor_copy(out=bsl[0:1, :], in_=bsf[0:1, :]), reads=[bsb, tmpb], writes=[bsb])
        s1 = A.alloc(64 * 4, F32)
        s2 = A.alloc(64 * 4, F32)
        st = A.alloc(5 * 8 * 4, F32, (5, 8))
        junk = A.alloc(256 * 2, BF16)
        sb_ = Buf()
        junkb = Buf()
        ub = A.alloc(2 * T * 2, BF16, (2, T))
        ubb = [Buf(), Buf()]
        for vv in range(8):
            Wv, Wb = self.slab_in(win, 2048 + vv * 256)
            for tp in range(4):
                pb, pbb = self.psalloc()
                for t2 in range(2):
                    tt = tp * 2 + t2
                    for kc in range(NCH):
                        self.mm(pb[:, t2 * 256:(t2 + 1) * 256], self.hT[:, kc, tt * 128:(tt + 1) * 128],
                                Wv[:, kc, :], kc == 0, kc == NCH - 1, reads=[Wb, self.hb[kc]], writes=[pbb])
                for t2 in range(2):
                    tt = tp * 2 + t2
                    vs = vt[:, tt, vv * 256:(vv + 1) * 256]
                    S.op("act", lambda e, o=vs, i=pb[:, t2 * 256:(t2 + 1) * 256], a=s1[:, tt * 8 + vv:tt * 8 + vv + 1]:
                         e.activation(out=o, in_=i, func=AF.Gelu, accum_out=a),
                         reads=[pbb], writes=[vtb[tt], sb_], extra=bar)
                    S.op("act", lambda e, i=vs, a=s2[:, tt * 8 + vv:tt * 8 + vv + 1]:
                         e.activation(out=junk, in_=i, func=AF.Square, accum_out=a),
                         reads=[vtb[tt]], writes=[junkb, sb_], extra=bar)
        S1, S2, mean, rstd, msq = (st[:, k, :] for k in range(5))
        S.op("dve", lambda e: e.tensor_reduce(out=S1, in_=s1.rearrange("p (a b) -> p a b", b=8), axis=AX.X, op=ALU.add),
             reads=[sb_], writes=[sb_])
        S.op("dve", lambda e: e.tensor_reduce(out=S2, in_=s2.rearrange("p (a b) -> p a b", b=8), axis=AX.X, op=ALU.add),
             reads=[sb_], writes=[sb_])
        S.op("dve", lambda e: e.tensor_scalar(out=mean, in0=S1, scalar1=1.0 / D, scalar2=None, op0=ALU.mult),
             reads=[sb_], writes=[sb_])
        S.op("dve", lambda e: e.tensor_tensor(out=msq, in0=mean, in1=mean, op=ALU.mult), reads=[sb_], writes=[sb_])
        S.op("dve", lambda e: e.scalar_tensor_tensor(out=rstd, in0=S2, scalar=1.0 / D, in1=msq,
                                                     op0=ALU.mult, op1=ALU.subtract), reads=[sb_], writes=[sb_])
        S.op("act", lambda e: e.activation(out=rstd, in_=rstd, func=AF.Sqrt, bias=self.epsc[:, 0:1]),
             reads=[sb_, self.constb], writes=[sb_])
        S.op("dve", lambda e: e.reciprocal(out=rstd, in_=rstd), reads=[sb_], writes=[sb_])
        for tt in range(8):
            for half in range(2):
                vs = vt[:, tt, half * T:(half + 1) * T]
                S.op("dve", lambda e, i=vs, a=mean[:, tt:tt + 1], b=rstd[:, tt:tt + 1]:
                     e.tensor_scalar(out=tmp, in0=i, scalar1=a, scalar2=b, op0=ALU.subtract, op1=ALU.mult),
                     reads=[vtb[tt], sb_], writes=[tmpb])
                S.op("dve", lambda e, g=grep_[:, half * T:(half + 1) * T]:
                     e.tensor_tensor(out=tmp, in0=tmp, in1=g, op=ALU.mult), reads=[tmpb, gbb], writes=[tmpb])
                S.op("dve", lambda e, o=vs, b=brep[:, half * T:(half + 1) * T]:
                     e.tensor_tensor(out=o, in0=tmp, in1=b, op=ALU.add), reads=[tmpb, gbb], writes=[vtb[tt]])
        for uu in range(8):
            Wv, Wb = self.slab_in(win, uu * 256)
            for ii in range(2):
                for half in range(2):
                    pb, pbb = self.psalloc()
                    for kc in range(NCH):
                        self.mm(pb[:, :], Wv[:, kc, ii * 128:(ii + 1) * 128],
                                self.hT[:, kc, half * 512:(half + 1) * 512], kc == 0, kc == NCH - 1,
                                reads=[Wb, self.hb[kc]], writes=[pbb])
                    S.op("act", lambda e, o=ub[:, ii, half * 512:(half + 1) * 512], i=pb[:, :]:
                         e.activation(out=o, in_=i, func=AF.Gelu), reads=[pbb], writes=[ubb[ii]], extra=bar)
            for ii in range(2):
                chn = uu * 2 + ii
                for half in range(2):
                    pb, pbb = self.psalloc()
                    for t4 in range(4):
                        tt = half * 4 + t4
                        o = pb[:, t4 * 128:(t4 + 1) * 128]
                        self.mm(o, vt[:, tt, chn * 128:(chn + 1) * 128], wmT[:, uu, :], True, False,
                                reads=[vtb[tt], wmb], writes=[pbb])
                        self.mm(o, self.ones[0:1, :], bsh[0:1, uu * 128:(uu + 1) * 128], False, False,
                                reads=[self.constb, bsb], writes=[pbb])
                        self.mm(o, self.ones[0:1, :], bsl[0:1, uu * 128:(uu + 1) * 128], False, True,
                                reads=[self.constb, bsb], writes=[pbb])
                    S.op("dve", lambda e, o=ub[:, ii, half * 512:(half + 1) * 512], i=pb[:, :]:
                         e.tensor_tensor(out=o, in0=i, in1=o, op=ALU.mult),
                         reads=[pbb, ubb[ii]], writes=[ubb[ii]])
            Ov, Ob = self.slab_rows(wout, uu * 256)
            self.proj_out(Ov, Ob, 2, [ub[:, 0, :], ub[:, 1, :]], ubb)
        A.release(m)

    def build(self):
        nc = self.nc
        S = self.S
        self.declare()
        with contextlib.ExitStack() as es:
            def sb(name, shape, dt):
                return es.enter_context(nc.sbuf_tensor("sb_" + name, shape, dt))

            self.xT = sb("xT", [P, NCH, T], F32)
            self.hT = sb("hT", [P, NCH, T], BF16)
            self.ring = [sb(f"ring{i}", [P, SLAB], BF16) for i in range(NSLOT)]
            self.sq = sb("sq", [P, 2, T], BF16)
            self.rs = sb("rs", [P, T], F32)
            self.ones = sb("ones", [P, P], BF16)
            self.gmix = sb("gmix", [P, self.nl * NCH], F32)
            self.gffn = sb("gffn", [P, self.nl * NCH], F32)
            self.gfin = sb("gfin", [P, NCH], F32)
            self.flags = sb("flags", [P, 2], F32)
            self.epsc = sb("epsc", [P, 1], F32)
            self.zeroc = sb("zeroc", [P, 1], F32)
            self.convw = sb("convw", [P, max(1, self.ne) * 24], F32)
            ARENA_BYTES = 70 * 1024
            arena_t = sb("arena", [P, ARENA_BYTES // 2], BF16)
            self.arena = Arena(arena_t, ARENA_BYTES)
            self.ps = [es.enter_context(nc.psum_tensor(f"ps{i}", [P, 512], F32)) for i in range(8)]
            self.psb = [Buf() for _ in range(8)]
            self.ringb = [Buf() for _ in range(NSLOT)]
            self.xb = [[Buf(), Buf()] for _ in range(NCH)]
            self.hb = [Buf() for _ in range(NCH)]
            self.sqb = [Buf(), Buf()]
            self.rsb = [Buf(), Buf()]
            self.constb = Buf()

            xv = self.d_x.rearrange("(c p) t -> p c t", p=P)
            for k in range(4):
                S.dma("sp", f"xin{k}", lambda e, k=k: e.dma_start(out=self.xT[:, 4 * k:4 * k + 4, :],
                                                                   in_=xv[:, 4 * k:4 * k + 4, :]),
                      writes=[b for c in range(4 * k, 4 * k + 4) for b in self.xb[c]])
            cl = [("cg0", self.gmix, self.d_gmix), ("cg1", self.gffn, self.d_gffn), ("cg2", self.gfin, self.d_gfin),
                  ("cg3", self.flags, self.d_flag)]
            if self.ne:
                cl.append(("cg4", self.convw, self.d_cw))
            ctoks = []
            for key, dst, src in cl:
                ctoks.append(S.dma("act", key, lambda e, o=dst, i=src: e.dma_start(out=o[:, :], in_=i[:, :])))
            ctoks.append(S.op("dve", lambda e: e.memset(self.ones[:, :], 1.0)))
            ctoks.append(S.op("dve", lambda e: e.memset(self.epsc[:, :], EPS)))
            ctoks.append(S.op("dve", lambda e: e.memset(self.zeroc[:, :], 0.0)))
            for t in ctoks:
                self.constb.w[t[0]] = max(self.constb.w.get(t[0], 0), t[1])

            outs = []
            ei = oi = 0
            import os
            dbg = os.environ.get("K_DBG", "")
            for li, l in enumerate(self.layers):
                if "nomix" not in dbg:
                    self.rmsnorm(self.gmix[:, li * NCH:(li + 1) * NCH])
                    if l % 2 == 0:
                        self.mixer_even(ei, self.gmix[:, li * NCH:(li + 1) * NCH])
                        ei += 1
                    else:
                        self.mixer_odd(oi)
                        oi += 1
                if "noffn" not in dbg:
                    self.rmsnorm(self.gffn[:, li * NCH:(li + 1) * NCH])
                    self.ffn(li)
            if self.last:
                self.obuf = self.arena.alloc(2 * T * 4, F32, (2, T))
                self.obb = [Buf(), Buf()]
                outs = self.rmsnorm(self.gfin, final=True)
            else:
                yv = self.d_out.rearrange("(c p) t -> p c t", p=P)
                for k in range(4):
                    outs.append(S.dma("sp", f"out{k}", lambda e, k=k: e.dma_start(
                        out=yv[:, 4 * k:4 * k + 4, :], in_=self.xT[:, 4 * k:4 * k + 4, :]),
                        reads=[b for c in range(4 * k, 4 * k + 4) for b in self.xb[c]]))
            S.wait_only("sp", outs)

            sems = {k: es.enter_context(nc.semaphore(f"s_{k}")) for k in sorted(S.keys)}
            block = es.enter_context(nc.Block())

            @block.tensor
            def _(e):
                S.replay("pe", e, sems)

            @block.scalar
            def _(e):
                S.replay("act", e, sems)

            @block.vector
            def _(e):
                S.replay("dve", e, sems)

            @block.gpsimd
            def _(e):
                S.replay("pool", e, sems)

            @block.sync
            def _(e):
                S.replay("sp", e, sems)
        return nc


def _bias_tables(rel_bias):
    k = np.arange(P)[:, None]
    q = np.arange(640)[None, :]
    idx = np.clip(q - k, -256, 256) + 256
    g = rel_bias[:, :, idx]
    ne = rel_bias.shape[0]
    g = g.reshape(ne, 4, 2, P, 640).transpose(0, 1, 3, 2, 4)
    return np.ascontiguousarray(g.reshape(ne * 4, P, 2 * 640)).astype(np.float32)


def _mask_neg():
    k = np.arange(P)[:, None] // 64
    q = np.arange(640)[None, :] // 64
    valid = (q - k >= 0) & (q - k <= 8)
    return np.where(valid, 0.0, NEG).astype(np.float32)


def _pm(v):
    v = np.asarray(v, dtype=np.float32).reshape(-1, NCH, P)
    return np.ascontiguousarray(v.transpose(2, 0, 1).reshape(P, -1))


_NC_CACHE = {}


def _run_segment(layers, first, last, xT_shards, inp, xh_shards=None):
    key = (tuple(layers), first, last)
    if key not in _NC_CACHE:
        _NC_CACHE[key] = Builder(layers, first, last).build()
    nc = _NC_CACHE[key]
    L = list(layers)
    ev = [l // 2 for l in L if l % 2 == 0]
    od = [l // 2 for l in L if l % 2 == 1]
    shared = {
        "g_mix": _pm(inp["mix_norm"][L]),
        "g_ffn": _pm(inp["ffn_norm"][L]),
        "g_fin": _pm(inp["final_norm"][None, :]),
    }
    import os
    if "noffn" not in os.environ.get("K_DBG", ""):
        shared["w_gate"] = np.ascontiguousarray(inp["ffn_w_gate"][L])
        shared["w_up"] = np.ascontiguousarray(inp["ffn_w_up"][L])
        shared["w_down"] = np.ascontiguousarray(inp["ffn_w_down"][L])
    if ev:
        shared["ab_w_in"] = np.ascontiguousarray(inp["ab_w_in"][ev])
        shared["ab_w_out"] = np.ascontiguousarray(inp["ab_w_out"][ev])
        shared["ab_biasg"] = _bias_tables(np.asarray(inp["ab_rel_bias"])[ev])
        shared["ab_maskneg"] = _mask_neg()
        cw = np.asarray(inp["ab_conv_w"])[ev].reshape(len(ev), 3, 8, P)
        shared["ab_convw"] = np.ascontiguousarray(cw.transpose(3, 0, 1, 2).reshape(P, len(ev) * 24))
    if od:
        shared["c_w_in"] = np.ascontiguousarray(inp["c_w_in"][od])
        shared["c_w_out"] = np.ascontiguousarray(inp["c_w_out"][od])
        shared["c_ln_g_rep"] = np.ascontiguousarray(np.broadcast_to(inp["c_ln_g"][od][:, None, :], (len(od), P, D)))
        shared["c_ln_b_rep"] = np.ascontiguousarray(np.broadcast_to(inp["c_ln_b"][od][:, None, :], (len(od), P, D)))
        shared["c_w_sT"] = np.ascontiguousarray(np.asarray(inp["c_w_s"])[od].transpose(0, 1, 3, 2))
        shared["c_b_s"] = np.ascontiguousarray(np.asarray(inp["c_b_s"])[od].reshape(len(od), 1, 1024))
    in_maps = []
    for c in range(N_CORES):
        m = dict(shared)
        m["xT_in"] = xT_shards[c]
        if ev:
            m["xh_in"] = xh_shards[c]
        fl = np.zeros((P, 2), np.float32)
        if c % 2 == 1:
            fl[:, 0] = 1.0
            fl[:, 1] = 0.0
        else:
            fl[:, 0] = 0.0
            fl[:, 1] = NEG
        m["flags"] = fl
        in_maps.append(m)
    res = run_bass_kernel_spmd(nc, in_maps, core_ids=list(range(N_CORES)))
    return [np.asarray(r["yT_out"]) for r in res.results]


SEGMENTS = [[0, 1], [2, 3]]


def _halos(shards):
    out = []
    for c in range(N_CORES):
        if c % 2 == 1:
            out.append(np.ascontiguousarray(shards[c - 1][None, :, T - HALO:T]))
        else:
            out.append(np.zeros((1, D, HALO), np.float32))
    return out


def kernel(**inputs):
    inp = {k: np.asarray(v) for k, v in inputs.items()}
    x = inp["x"].astype(np.float32, copy=False)
    B, Sq, _ = x.shape
    shards = []
    for c in range(N_CORES):
        b, hf = c // 2, c % 2
        shards.append(np.ascontiguousarray(x[b, hf * T:(hf + 1) * T, :].T))
    for si, seg in enumerate(SEGMENTS):
        shards = _run_segment(seg, si == 0, si == len(SEGMENTS) - 1, shards, inp, _halos(shards))
    out = np.empty((B, Sq, D), np.float32)
    for c in range(N_CORES):
        b, hf = c // 2, c % 2
        out[b, hf * T:(hf + 1) * T, :] = shards[c].T
    return out
```

```python
import contextlib
import numpy as np
import concourse.bass as bass
import concourse.mybir as mybir
from concourse.bass_utils import run_bass_kernel_spmd

F32 = mybir.dt.float32
BF16 = mybir.dt.bfloat16
AF = mybir.ActivationFunctionType
ALU = mybir.AluOpType
AX = mybir.AxisListType

P = 128
D = 2048
NCH = 16
T = 1024
HALO = 512
FF = 5632
NHC = 44
DEPTH = 4
EPS = 1e-6
NSLOT = 3
SLAB = 4096
NEG = -30000.0
SCALE = 128 ** -0.5
N_CORES = 8
import os as _os
NROT = int(_os.environ.get('K_NROT', '6'))


class Buf:
    __slots__ = ("w", "r")

    def __init__(self):
        self.w = {}
        self.r = {}


class Sched:
    ENGS = ("pe", "act", "dve", "pool", "sp")

    def __init__(self):
        self.q = {e: [] for e in self.ENGS}
        self.cnt = {}
        self.seen = {e: {} for e in self.ENGS}
        self.keys = set(self.ENGS)

    def _deps(self, eng, reads, writes, extra):
        need = {}

        def add(d):
            for k, v in d.items():
                if v > need.get(k, 0):
                    need[k] = v

        for b in reads:
            add(b.w)
        for b in writes:
            add(b.w)
            add(b.r)
        for t in extra:
            if t is not None:
                add({t[0]: t[1]})
        waits = []
        seen = self.seen[eng]
        for k, v in need.items():
            if seen.get(k, 0) < v:
                waits.append((k, v))
                seen[k] = v
        return waits

    def raw(self, queue, key, inc, fn, reads=(), writes=(), extra=()):
        waits = self._deps(queue, reads, writes, extra)
        self.keys.add(key)
        self.cnt[key] = self.cnt.get(key, 0) + inc
        tok = (key, self.cnt[key])
        self.q[queue].append((waits, fn, key, inc))
        for b in reads:
            if tok[1] > b.r.get(key, 0):
                b.r[key] = tok[1]
        for b in writes:
            b.w = {key: tok[1]}
            b.r = {}
        return tok

    def op(self, eng, fn, reads=(), writes=(), extra=()):
        return self.raw(eng, eng, 1, fn, reads, writes, extra)

    def dma(self, queue, key, fn, reads=(), writes=(), extra=()):
        return self.raw(queue, key, 16, fn, reads, writes, extra)

    def wait_only(self, eng, toks):
        waits = self._deps(eng, (), (), toks)
        self.q[eng].append((waits, None, None, 0))

    def replay(self, eng, e, sems):
        for waits, fn, key, inc in self.q[eng]:
            for (k, v) in waits:
                e.wait_ge(sems[k], v)
            if fn is not None:
                fn(e).then_inc(sems[key], inc)


class Arena:
    def __init__(self, ap, nbytes):
        self.ap = ap
        self.n = nbytes
        self.off = 0

    def mark(self):
        return self.off

    def release(self, m):
        self.off = m

    def alloc(self, nbytes, dtype, shape=None):
        nbytes = (nbytes + 31) // 32 * 32
        o = self.off
        if o + nbytes > self.n:
            raise RuntimeError(f"arena overflow: {o}+{nbytes} > {self.n}")
        self.off = o + nbytes
        v = self.ap[:, o // 2:(o + nbytes) // 2]
        if dtype == F32:
            v = v.bitcast(F32)
        if shape is not None:
            if len(shape) == 2:
                v = v.rearrange("p (a b) -> p a b", b=shape[1])
            elif len(shape) == 3:
                v = v.rearrange("p (a b c) -> p a b c", b=shape[1], c=shape[2])
        return v


class Builder:
    def __init__(self, layers, first, last, plan=None):
        self.layers = list(layers)
        self.first = first
        self.last = last
        self.S = Sched()
        self.nc = bass.Bass("TRN2", target_bir_lowering=False)
        self.slab_i = 0
        self.ps_i = 0
        self.barrier = None
        self.last_x = None
        self.nrot = 8
        self.plan = plan
        self.slab_log = []
        self.dma_issued = 0
        self.cast_issued = 0
        self.dram = {}

    def declare(self):
        nc = self.nc
        L = self.layers
        ne = len([l for l in L if l % 2 == 0])
        no = len([l for l in L if l % 2 == 1])
        nl = len(L)
        self.ne, self.no, self.nl = ne, no, nl

        def din(name, shape, dt=F32):
            ap = nc.dram_tensor(name, list(shape), dt, kind="ExternalInput").ap()
            self.dram[name] = ap
            return ap

        self.d_x = din("xT_in", [D, T])
        self.d_out = nc.dram_tensor("yT_out", [D, T], F32, kind="ExternalOutput").ap()
        self.d_gmix = din("g_mix", [P, nl * NCH])
        self.d_gffn = din("g_ffn", [P, nl * NCH])
        self.d_gfin = din("g_fin", [P, NCH])
        self.d_flag = din("flags", [P, 2])
        import os
        self.dbg = os.environ.get("K_DBG", "")
        if "noffn" not in self.dbg:
            self.d_wg = din("w_gate", [nl, D, FF])
            self.d_wu = din("w_up", [nl, D, FF])
            self.d_wd = din("w_down", [nl, FF, D])
        if ne:
            self.d_win = din("ab_w_in", [ne, D, 6144])
            self.d_wout = din("ab_w_out", [ne, D, D])
            self.d_bias = din("ab_biasg", [ne * 4, P, 2 * 640])
            self.d_mask = din("ab_maskneg", [P, 640])
            self.d_cw = din("ab_convw", [P, ne * 24])
            self.d_hin = [nc.dram_tensor(f"hin{i}", [D, HALO], BF16) for i in range(ne)]
            self.d_hout = [nc.dram_tensor(f"hout{i}", [2 * D, HALO], BF16) for i in range(ne)]
        if no:
            self.d_cwin = din("c_w_in", [no, D, 4096])
            self.d_cwout = din("c_w_out", [no, D, D])
            self.d_lng = din("c_ln_g_rep", [no, P, D])
            self.d_lnb = din("c_ln_b_rep", [no, P, D])
            self.d_wsT = din("c_w_sT", [no, 8, P, P])
            self.d_bs = din("c_b_s", [no, 1, 1024])

    def psalloc(self):
        i = self.ps_i % self.nrot
        self.ps_i += 1
        return self.ps[i], self.psb[i]

    def _slab_views(self, desc, s):
        kind, name, idx, off = desc
        w = self.dram[name][idx]
        if kind == "in":
            src = w.rearrange("(c p) n -> p c n", p=P)[:, :, off:off + 256]
            dst = self.ring32[s][:, :].rearrange("p (c n) -> p c n", n=256)
            view = self.ring16[s][:, 0:SLAB].rearrange("p (c n) -> p c n", n=256)
        else:
            src = w[off:off + 256, :].rearrange("(k p) n -> p k n", p=P)
            dst = self.ring32[s][:, :].rearrange("p (k n) -> p k n", n=2048)
            view = self.ring16[s][:, 0:SLAB].rearrange("p (k n) -> p k n", n=2048)
        return src, dst, view

    def _issue_dma(self, k, desc):
        s = k % NSLOT
        src, dst, _ = self._slab_views(desc, s)
        self.S.dma("sp", f"slot{s}", lambda e, o=dst, i=src: e.dma_start(out=o, in_=i), writes=[self.ringb[s]])

    def _issue_cast(self, k):
        s = k % NSLOT
        o = self.ring16[s][:, 0:SLAB]
        i = self.ring32[s][:, :]
        self.S.op("act", lambda e, o=o, i=i: e.activation(out=o, in_=i, func=AF.Copy),
                  reads=[self.ringb[s]], writes=[self.ringb[s]])

    def _get_slab(self, desc, keep_prev=0):
        k = self.slab_i
        self.slab_i += 1
        s = k % NSLOT
        if self.plan is None:
            self.slab_log.append(desc)
            self._issue_dma(k, desc)
            self._issue_cast(k)
        else:
            assert self.plan[k] == desc, (k, self.plan[k], desc)
            n = len(self.plan)
            while self.dma_issued < min(k - keep_prev + NSLOT, n):
                self._issue_dma(self.dma_issued, self.plan[self.dma_issued])
                self.dma_issued += 1
            while self.cast_issued < min(k + 2, n):
                self._issue_cast(self.cast_issued)
                self.cast_issued += 1
            if k > 0 and self.S.cnt.get("pe", 0) > 0:
                self.S.wait_only("pool", [("pe", self.S.cnt["pe"])])
        return self._slab_views(desc, s)[2], self.ringb[s]

    def slab_in(self, wname, idx, c0):
        return self._get_slab(("in", wname, idx, c0))

    def slab_rows(self, wname, idx, r0, keep_prev=0):
        return self._get_slab(("rows", wname, idx, r0), keep_prev)

    def mm(self, out, lhsT, rhs, start, stop, reads, writes):
        return self.S.op("pe", lambda e: e.matmul(out, lhsT=lhsT, rhs=rhs, start=start, stop=stop),
                         reads=reads, writes=writes)

    def xadd(self, dc, half, pb, pbb):
        xs = self.xT[:, dc, half * 512:(half + 1) * 512]
        self.last_x = self.S.op("dve", lambda e: e.tensor_tensor(out=xs, in0=xs, in1=pb[:, :], op=ALU.add),
                                reads=[pbb], writes=[self.xb[dc][half]])
        return self.last_x

    def proj_out(self, Wv, Wb, nk, acts, actb):
        tok = None
        for dc in range(NCH):
            for half in range(2):
                pb, pbb = self.psalloc()
                for k in range(nk):
                    self.mm(pb[:, :], Wv[:, k, dc * 128:(dc + 1) * 128],
                            acts[k][:, half * 512:(half + 1) * 512], k == 0, k == nk - 1,
                            reads=[Wb, actb[k]], writes=[pbb])
                tok = self.xadd(dc, half, pb, pbb)
        return tok

    def rmsnorm(self, gain, final=False):
        S = self.S
        A = self.arena
        m = A.mark()
        nb = [self.last_x]
        self.sq = A.alloc(2 * T * 2, BF16, (2, T))
        self.rs = A.alloc(T * 4, F32)
        self.sqb = [Buf(), Buf()]
        self.rsb = [Buf(), Buf()]
        if final:
            self.obuf = A.alloc(2 * T * 4, F32, (2, T))
            self.obb = [Buf(), Buf()]
        pa = [self.psalloc(), self.psalloc()]
        for c in range(NCH):
            sq = self.sq[:, c % 2, :]
            sqb = self.sqb[c % 2]
            S.op("act", lambda e, o=sq, i=self.xT[:, c, :]: e.activation(out=o, in_=i, func=AF.Square),
                 reads=self.xb[c], writes=[sqb], extra=nb)
            for half in range(2):
                self.mm(pa[half][0][:, :], self.ones[:, :], sq[:, half * 512:(half + 1) * 512],
                        c == 0, c == NCH - 1, reads=[sqb, self.constb], writes=[pa[half][1]])
        for half in range(2):
            rs = self.rs[:, half * 512:(half + 1) * 512]
            S.op("act", lambda e, o=rs, i=pa[half][0][:, :]: e.activation(
                out=o, in_=i, func=AF.Sqrt, bias=self.epsc[:, 0:1], scale=1.0 / D),
                reads=[pa[half][1], self.constb], writes=[self.rsb[half]])
            S.op("dve", lambda e, o=rs: e.reciprocal(out=o, in_=o),
                 reads=[self.rsb[half]], writes=[self.rsb[half]])
        tok = None
        outs = []
        for c in range(NCH):
            if not final:
                o = self.hT[:, c, :]
                tok = S.op("dve", lambda e, o=o, i=self.xT[:, c, :], g=gain[:, c:c + 1]:
                           e.scalar_tensor_tensor(out=o, in0=i, scalar=g, in1=self.rs[:, :],
                                                  op0=ALU.mult, op1=ALU.mult),
                           reads=self.xb[c] + self.rsb + [self.constb], writes=[self.hb[c]])
            else:
                ob = self.obuf[:, c % 2, :]
                S.op("dve", lambda e, o=ob, i=self.xT[:, c, :], g=gain[:, c:c + 1]:
                     e.scalar_tensor_tensor(out=o, in0=i, scalar=g, in1=self.rs[:, :],
                                            op0=ALU.mult, op1=ALU.mult),
                     reads=self.xb[c] + self.rsb + [self.constb], writes=[self.obb[c % 2]], extra=nb)
                dst = self.d_out[c * 128:(c + 1) * 128, :]
                outs.append(S.dma("sp", f"out{c % 2}", lambda e, o=dst, i=ob: e.dma_start(out=o, in_=i),
                                  reads=[self.obb[c % 2]]))
        self.barrier = tok
        A.release(m)
        return outs

    def ffn(self, li):
        S = self.S
        A = self.arena
        m = A.mark()
        self.nrot = 8
        act = A.alloc(4 * T * 2, BF16, (4, T))
        actb = [Buf() for _ in range(4)]
        sg = A.alloc(4 * 512 * 4, F32, (4, 512))
        sgb = [Buf() for _ in range(4)]
        for grp in range(NHC // 4):
            for jj in range(2):
                j = grp * 2 + jj
                Gv, Gb = self.slab_in("w_gate", li, j * 256)
                pg = [[self.psalloc(), self.psalloc()] for _ in range(2)]
                for i in range(2):
                    for kc in range(NCH):
                        for half in range(2):
                            self.mm(pg[i][half][0][:, :], Gv[:, kc, i * 128:(i + 1) * 128],
                                    self.hT[:, kc, half * 512:(half + 1) * 512], kc == 0, kc == NCH - 1,
                                    reads=[Gb, self.hb[kc]], writes=[pg[i][half][1]])
                for i in range(2):
                    for half in range(2):
                        S.op("act", lambda e, o=sg[:, i * 2 + half, :], i_=pg[i][half][0][:, :]:
                             e.activation(out=o, in_=i_, func=AF.Silu),
                             reads=[pg[i][half][1]], writes=[sgb[i * 2 + half]], extra=[self.barrier])
                Uv, Ub = self.slab_in("w_up", li, j * 256)
                pu = [[self.psalloc(), self.psalloc()] for _ in range(2)]
                for i in range(2):
                    for kc in range(NCH):
                        for half in range(2):
                            self.mm(pu[i][half][0][:, :], Uv[:, kc, i * 128:(i + 1) * 128],
                                    self.hT[:, kc, half * 512:(half + 1) * 512], kc == 0, kc == NCH - 1,
                                    reads=[Ub, self.hb[kc]], writes=[pu[i][half][1]])
                for i in range(2):
                    hc = jj * 2 + i
                    for half in range(2):
                        S.op("dve", lambda e, o=act[:, hc, half * 512:(half + 1) * 512], a=sg[:, i * 2 + half, :],
                             b=pu[i][half][0][:, :]: e.tensor_tensor(out=o, in0=a, in1=b, op=ALU.mult),
                             reads=[sgb[i * 2 + half], pu[i][half][1]], writes=[actb[hc]])
            D0v, D0b = self.slab_rows("w_down", li, (grp * 4) * 128)
            D1v, D1b = self.slab_rows("w_down", li, (grp * 4 + 2) * 128, keep_prev=1)
            for dc in range(NCH):
                for half in range(2):
                    pb, pbb = self.psalloc()
                    for k in range(4):
                        Dv, Db = (D0v, D0b) if k < 2 else (D1v, D1b)
                        self.mm(pb[:, :], Dv[:, k % 2, dc * 128:(dc + 1) * 128],
                                act[:, k, half * 512:(half + 1) * 512], k == 0, k == 3,
                                reads=[Db, actb[k]], writes=[pbb])
                    self.xadd(dc, half, pb, pbb)
        A.release(m)

    def mixer_even(self, ei):
        S = self.S
        nc = self.nc
        A = self.arena
        m = A.mark()
        bar = [self.barrier]
        hh_t = A.alloc(NCH * HALO * 2, BF16, (NCH, HALO))
        hhb = Buf()
        hin_v = self.d_hin[ei].ap().rearrange("(c p) n -> p c n", p=P)
        hout_v = self.d_hout[ei].ap().rearrange("(c p) n -> p c n", p=P)
        tw = S.dma("sp", "hw", lambda e: e.dma_start(out=hin_v, in_=self.hT[:, :, T - HALO:T]),
                   reads=self.hb)
        hin_t, hout_t = self.d_hin[ei], self.d_hout[ei]
        tc = S.raw("pool", f"cc{ei}", 1, lambda e: e.collective_compute(
            "AllGather", ALU.bypass, replica_groups=[[0, 1], [2, 3], [4, 5], [6, 7]],
            ins=[hin_t.ap().opt()], outs=[hout_t.ap().opt()]), extra=[tw])
        S.dma("sp", "hr", lambda e: e.dma_start(out=hh_t, in_=hout_v[:, 0:NCH, :]),
              writes=[hhb], extra=[tc] + bar)
        mask = A.alloc(640 * 4, F32)
        maskb = Buf()
        S.dma("sp", "cm", lambda e: e.dma_start(out=mask, in_=self.d_mask[:, :]), writes=[maskb], extra=bar)
        m2 = A.mark()
        qT = A.alloc(2 * T * 2, BF16, (2, T))
        kT = A.alloc(2 * (T + HALO) * 2, BF16, (2, T + HALO))
        Vt = A.alloc(12 * 256 * 2, BF16, (12, 256))
        tb = A.alloc(2 * 640 * 4, F32, (2, 640))
        tmp = A.alloc(2 * 640 * 4, F32, (2, 640))
        Pb = A.alloc(6 * 640 * 2, BF16, (6, 640))
        rinv = A.alloc(512 * 4, F32)
        ao = A.alloc(2 * T * 2, BF16, (2, T))
        qb = [Buf(), Buf()]
        kb = [Buf(), Buf()]
        Vb = [Buf() for _ in range(12)]
        tbb = Buf()
        tmpb = [Buf(), Buf()]
        Pbb = [Buf() for _ in range(6)]
        rinvb = Buf()
        aob = [Buf(), Buf()]
        ntmp = 0
        last = None
        for g in range(0 if "noattn" not in self.dbg else 4, 4):
            self.nrot = 6
            Qv, Qb = self.slab_in("ab_w_in", ei, g * 256)
            S.dma("sp", "tb", lambda e, g=g: e.dma_start(out=tb, in_=self.d_bias[ei * 4 + g].rearrange(
                "p (a b) -> p a b", b=640)), writes=[tbb], extra=bar + [last])
            for hh in range(2):
                S.op("dve", lambda e, o=tb[:, hh, :]: e.tensor_tensor(out=o, in0=o, in1=mask, op=ALU.add),
                     reads=[maskb, tbb], writes=[tbb])
            for hh in range(2):
                for half in range(2):
                    pb, pbb = self.psalloc()
                    for kc in range(NCH):
                        self.mm(pb[:, :], Qv[:, kc, hh * 128:(hh + 1) * 128],
                                self.hT[:, kc, half * 512:(half + 1) * 512], kc == 0, kc == NCH - 1,
                                reads=[Qb, self.hb[kc]], writes=[pbb])
                    S.op("act", lambda e, o=qT[:, hh, half * 512:(half + 1) * 512], i=pb[:, :]:
                         e.activation(out=o, in_=i, func=AF.Copy),
                         reads=[pbb], writes=[qb[hh]], extra=bar + [last])
            Kv, Kb = self.slab_in("ab_w_in", ei, 1024 + g * 256)
            for hh in range(2):
                for seg in (1, 2, 0):
                    pb, pbb = self.psalloc()
                    for kc in range(NCH):
                        if seg == 0:
                            rhs, rb = hh_t[:, kc, :], hhb
                        else:
                            rhs, rb = self.hT[:, kc, (seg - 1) * 512:seg * 512], self.hb[kc]
                        self.mm(pb[:, :], Kv[:, kc, hh * 128:(hh + 1) * 128], rhs, kc == 0, kc == NCH - 1,
                                reads=[Kb, rb], writes=[pbb])
                    S.op("act", lambda e, o=kT[:, hh, seg * 512:(seg + 1) * 512], i=pb[:, :]:
                         e.activation(out=o, in_=i, func=AF.Copy),
                         reads=[pbb], writes=[kb[hh]], extra=bar + [last])
            Vv, Vvb = self.slab_in("ab_w_in", ei, 2048 + g * 256)
            for tp in (2, 3, 4, 5, 0, 1):
                pb, pbb = self.psalloc()
                for t2 in range(2):
                    tt = tp * 2 + t2
                    for kc in range(NCH):
                        if tt < 4:
                            lh, lb = hh_t[:, kc, tt * 128:(tt + 1) * 128], hhb
                        else:
                            lh, lb = self.hT[:, kc, (tt - 4) * 128:(tt - 3) * 128], self.hb[kc]
                        self.mm(pb[:, t2 * 256:(t2 + 1) * 256], lh, Vv[:, kc, :], kc == 0, kc == NCH - 1,
                                reads=[Vvb, lb], writes=[pbb])
                S.op("dve", lambda e, o=Vt[:, tp * 2:tp * 2 + 2, :], i=pb[:, :].rearrange(
                    "p (a b) -> p a b", b=256): e.tensor_copy(out=o, in_=i),
                    reads=[pbb], writes=[Vb[tp * 2], Vb[tp * 2 + 1]], extra=bar + [last])
            for hh in range(2):
                pv = rsb_ = None
                for j in range(12 if "noscore" not in self.dbg else 0):
                    qc0 = max(0, 2 * j - 8)
                    qc1 = min(16, 2 * j + 2)
                    nq = (qc1 - qc0) * 64
                    q0 = qc0 * 64
                    c0 = (qc0 + 8 - 2 * j) * 64
                    tsel = ntmp % 2
                    ntmp += 1
                    for (a0, a1) in ((0, min(nq, 512)), (512, nq)):
                        if a1 <= a0:
                            continue
                        pb, pbb = self.psalloc()
                        self.mm(pb[:, 0:a1 - a0], kT[:, hh, j * 128:(j + 1) * 128],
                                qT[:, hh, q0 + a0:q0 + a1], True, True,
                                reads=[kb[hh], qb[hh]], writes=[pbb])
                        S.op("dve", lambda e, o=tmp[:, tsel, a0:a1], i=pb[:, 0:a1 - a0],
                             t=tb[:, hh, c0 + a0:c0 + a1]: e.scalar_tensor_tensor(
                                 out=o, in0=i, scalar=SCALE, in1=t, op0=ALU.mult, op1=ALU.add),
                             reads=[pbb, tbb], writes=[tmpb[tsel]])
                    bias_ap = self.flags[:, 1:2] if j < 4 else self.zeroc[:, 0:1]
                    S.op("act", lambda e, o=Pb[:, j % 6, 0:nq], i=tmp[:, tsel, 0:nq], b=bias_ap:
                         e.activation(out=o, in_=i, func=AF.Exp, bias=b),
                         reads=[tmpb[tsel], self.constb], writes=[Pbb[j % 6]])
                    if j >= 4 and "nopv" not in self.dbg:
                        p = j - 4
                        slot = p % 4
                        if slot == 0:
                            pv = (self.ps[6], self.psb[6])
                            rsb_ = (self.ps[7], self.psb[7])
                        for jj in range(p, p + 5):
                            col = (2 * p - max(0, 2 * jj - 8)) * 64
                            rhs = Pb[:, jj % 6, col:col + 128]
                            self.mm(pv[0][:, slot * 128:(slot + 1) * 128], Vt[:, jj, hh * 128:(hh + 1) * 128],
                                    rhs, jj == p, jj == p + 4, reads=[Vb[jj], Pbb[jj % 6]], writes=[pv[1]])
                            self.mm(rsb_[0][:, slot * 128:(slot + 1) * 128], self.ones[:, :],
                                    rhs, jj == p, jj == p + 4, reads=[self.constb, Pbb[jj % 6]],
                                    writes=[rsb_[1]])
                        if slot == 3:
                            rnd = p // 4
                            S.op("dve", lambda e, i=rsb_[0][:, :]: e.reciprocal(out=rinv, in_=i),
                                 reads=[rsb_[1]], writes=[rinvb])
                            S.op("dve", lambda e, o=ao[:, hh, rnd * 512:(rnd + 1) * 512], i=pv[0][:, :]:
                                 e.tensor_tensor(out=o, in0=i, in1=rinv, op=ALU.mult),
                                 reads=[pv[1], rinvb], writes=[aob[hh]], extra=bar + [last])
            Ov, Ob = self.slab_rows("ab_w_out", ei, g * 256)
            last = self.proj_out(Ov, Ob, 2, [ao[:, 0, :], ao[:, 1, :]], aob)
        A.release(m2)
        self.nrot = 8
        TZ = T + 8
        zz = A.alloc(2 * TZ * 4, F32, (2, TZ))
        y = A.alloc(2 * T * 4, F32, (2, T))
        co = A.alloc(2 * T * 2, BF16, (2, T))
        zb = [Buf(), Buf()]
        yb = [Buf(), Buf()]
        cob = [Buf(), Buf()]
        cbar = bar + [last]
        cw = self.convw[:, ei * 24:(ei + 1) * 24]
        for cc in range(0 if "noconv" not in self.dbg else 4, 4):
            Cv, Cb = self.slab_in("ab_w_in", ei, 4096 + cc * 256)
            for ii in range(2):
                pc = [self.psalloc(), self.psalloc()]
                for kc in range(NCH):
                    for half in range(2):
                        self.mm(pc[half][0][:, :], Cv[:, kc, ii * 128:(ii + 1) * 128],
                                self.hT[:, kc, half * 512:(half + 1) * 512], kc == 0, kc == NCH - 1,
                                reads=[Cb, self.hb[kc]], writes=[pc[half][1]])
                ph_ = self.psalloc()
                for kc in range(NCH):
                    self.mm(ph_[0][:, 0:2], Cv[:, kc, ii * 128:(ii + 1) * 128], hh_t[:, kc, HALO - 2:HALO],
                            kc == 0, kc == NCH - 1, reads=[Cb, hhb], writes=[ph_[1]])
                for half in range(2):
                    S.op("act", lambda e, o=zz[:, ii, 2 + half * 512:2 + (half + 1) * 512], i=pc[half][0][:, :]:
                         e.activation(out=o, in_=i, func=AF.Copy),
                         reads=[pc[half][1]], writes=[zb[ii]], extra=cbar)
                S.op("act", lambda e, o=zz[:, ii, 0:2], i=ph_[0][:, 0:2]: e.activation(out=o, in_=i, func=AF.Copy),
                     reads=[ph_[1]], writes=[zb[ii]], extra=cbar)
            Hv, Hb = self.slab_in("ab_w_in", ei, 5120 + cc * 256)
            for ii in range(2):
                ch = cc * 2 + ii
                ph = [self.psalloc(), self.psalloc()]
                for kc in range(NCH):
                    for half in range(2):
                        self.mm(ph[half][0][:, :], Hv[:, kc, ii * 128:(ii + 1) * 128],
                                self.hT[:, kc, half * 512:(half + 1) * 512], kc == 0, kc == NCH - 1,
                                reads=[Hb, self.hb[kc]], writes=[ph[half][1]])
                ph_ = self.psalloc()
                for kc in range(NCH):
                    self.mm(ph_[0][:, 0:2], Hv[:, kc, ii * 128:(ii + 1) * 128], hh_t[:, kc, HALO - 2:HALO],
                            kc == 0, kc == NCH - 1, reads=[Hb, hhb], writes=[ph_[1]])
                for half in range(2):
                    zs = zz[:, ii, 2 + half * 512:2 + (half + 1) * 512]
                    S.op("dve", lambda e, o=zs, b=ph[half][0][:, :]:
                         e.tensor_tensor(out=o, in0=o, in1=b, op=ALU.mult),
                         reads=[ph[half][1]], writes=[zb[ii]])
                S.op("dve", lambda e, o=zz[:, ii, 0:2], b=ph_[0][:, 0:2]:
                     e.scalar_tensor_tensor(out=o, in0=o, scalar=self.flags[:, 0:1], in1=b,
                                            op0=ALU.mult, op1=ALU.mult),
                     reads=[ph_[1], self.constb], writes=[zb[ii]])
                w0 = cw[:, 0 * 8 + ch:0 * 8 + ch + 1]
                w1 = cw[:, 1 * 8 + ch:1 * 8 + ch + 1]
                w2 = cw[:, 2 * 8 + ch:2 * 8 + ch + 1]
                ys = y[:, ii, :]
                S.op("dve", lambda e, w=w0, o=ys, i=zz[:, ii, 0:T]: e.tensor_scalar(
                    out=o, in0=i, scalar1=w, scalar2=None, op0=ALU.mult),
                    reads=[zb[ii], self.constb], writes=[yb[ii]], extra=cbar)
                S.op("dve", lambda e, w=w1, o=ys, i=zz[:, ii, 1:T + 1]: e.scalar_tensor_tensor(
                    out=o, in0=i, scalar=w, in1=o, op0=ALU.mult, op1=ALU.add),
                    reads=[zb[ii], self.constb], writes=[yb[ii]])
                S.op("dve", lambda e, w=w2, o=ys, i=zz[:, ii, 2:T + 2]: e.scalar_tensor_tensor(
                    out=o, in0=i, scalar=w, in1=o, op0=ALU.mult, op1=ALU.add),
                    reads=[zb[ii], self.constb], writes=[yb[ii]])
            Bv, Bb = self.slab_in("ab_w_in", ei, 3072 + cc * 256)
            for ii in range(2):
                pbq = [self.psalloc(), self.psalloc()]
                for kc in range(NCH):
                    for half in range(2):
                        self.mm(pbq[half][0][:, :], Bv[:, kc, ii * 128:(ii + 1) * 128],
                                self.hT[:, kc, half * 512:(half + 1) * 512], kc == 0, kc == NCH - 1,
                                reads=[Bb, self.hb[kc]], writes=[pbq[half][1]])
                for half in range(2):
                    S.op("dve", lambda e, o=co[:, ii, half * 512:(half + 1) * 512],
                         a=y[:, ii, half * 512:(half + 1) * 512], b=pbq[half][0][:, :]:
                         e.tensor_tensor(out=o, in0=a, in1=b, op=ALU.mult),
                         reads=[yb[ii], pbq[half][1]], writes=[cob[ii]], extra=cbar)
            Ov, Ob = self.slab_rows("ab_w_out", ei, 1024 + cc * 256)
            self.proj_out(Ov, Ob, 2, [co[:, 0, :], co[:, 1, :]], cob)
        A.release(m)

    def mixer_odd(self, oi):
        S = self.S
        A = self.arena
        self.nrot = 8
        m = A.mark()
        bar = [self.barrier]
        vt = A.alloc(8 * D * 2, BF16, (8, D))
        vtb = [Buf() for _ in range(8)]
        grep_ = A.alloc(D * 4, F32)
        brep = A.alloc(D * 4, F32)
        gbb = Buf()
        S.dma("sp", "lg", lambda e: e.dma_start(out=grep_, in_=self.d_lng[oi]), writes=[gbb], extra=bar)
        S.dma("sp", "lb", lambda e: e.dma_start(out=brep, in_=self.d_lnb[oi]), writes=[gbb], extra=bar)
        wmT = A.alloc(8 * P * 2, BF16, (8, P))
        wmb = Buf()
        tmp = A.alloc(512 * 4, F32)
        tmpb = Buf()
        ub = A.alloc(2 * T * 2, BF16, (2, T))
        ubb = [Buf(), Buf()]
        stg = ub.rearrange("p a b -> p (a b)").bitcast(F32)
        stgb = Buf()
        bsh = A.alloc(1024 * 2, BF16)
        bsl = A.alloc(1024 * 2, BF16)
        bsb = Buf()
        S.dma("sp", "wm", lambda e: e.dma_start(out=stg.rearrange("p (g t) -> p g t", t=P),
                                                in_=self.d_wsT[oi].rearrange("g s t -> s g t")),
              writes=[stgb], extra=bar)
        S.op("dve", lambda e: e.tensor_copy(out=wmT, in_=stg.rearrange("p (g t) -> p g t", t=P)),
             reads=[stgb], writes=[wmb], extra=bar)
        S.op("dve", lambda e: e.memset(wmT[64:128, :, 0:64], 0.0), writes=[wmb])
        bsf = stg
        S.dma("sp", "bs", lambda e: e.dma_start(out=bsf[0:1, :], in_=self.d_bs[oi]), writes=[stgb], extra=bar)
        S.op("dve", lambda e: e.tensor_copy(out=bsh[0:1, :], in_=bsf[0:1, :]), reads=[stgb], writes=[bsb], extra=bar)
        S.op("dve", lambda e: e.tensor_tensor(out=bsf[0:1, :], in0=bsf[0:1, :], in1=bsh[0:1, :], op=ALU.subtract),
             reads=[bsb, stgb], writes=[stgb])
        S.op("dve", lambda e: e.tensor_copy(out=bsl[0:1, :], in_=bsf[0:1, :]), reads=[stgb], writes=[bsb], extra=bar)
        s1 = A.alloc(64 * 4, F32)
        s2 = A.alloc(64 * 4, F32)
        st = A.alloc(5 * 8 * 4, F32, (5, 8))
        sb_ = Buf()
        junk = ub[:, 1, 0:256]
        junkb = stgb
        for vv in range(8):
            Wv, Wb = self.slab_in("c_w_in", oi, 2048 + vv * 256)
            for tp in range(4):
                pb, pbb = self.psalloc()
                for t2 in range(2):
                    tt = tp * 2 + t2
                    for kc in range(NCH):
                        self.mm(pb[:, t2 * 256:(t2 + 1) * 256], self.hT[:, kc, tt * 128:(tt + 1) * 128],
                                Wv[:, kc, :], kc == 0, kc == NCH - 1, reads=[Wb, self.hb[kc]], writes=[pbb])
                for t2 in range(2):
                    tt = tp * 2 + t2
                    vs = vt[:, tt, vv * 256:(vv + 1) * 256]
                    S.op("act", lambda e, o=vs, i=pb[:, t2 * 256:(t2 + 1) * 256], a=s1[:, tt * 8 + vv:tt * 8 + vv + 1]:
                         e.activation(out=o, in_=i, func=AF.Gelu, accum_out=a),
                         reads=[pbb], writes=[vtb[tt], sb_], extra=bar)
                    S.op("act", lambda e, i=vs, a=s2[:, tt * 8 + vv:tt * 8 + vv + 1]:
                         e.activation(out=junk, in_=i, func=AF.Square, accum_out=a),
                         reads=[vtb[tt]], writes=[junkb, sb_], extra=bar)
        S1, S2, mean, rstd, msq = (st[:, k, :] for k in range(5))
        S.op("dve", lambda e: e.tensor_reduce(out=S1, in_=s1.rearrange("p (a b) -> p a b", b=8), axis=AX.X, op=ALU.add),
             reads=[sb_], writes=[sb_])
        S.op("dve", lambda e: e.tensor_reduce(out=S2, in_=s2.rearrange("p (a b) -> p a b", b=8), axis=AX.X, op=ALU.add),
             reads=[sb_], writes=[sb_])
        S.op("dve", lambda e: e.tensor_scalar(out=mean, in0=S1, scalar1=1.0 / D, scalar2=None, op0=ALU.mult),
             reads=[sb_], writes=[sb_])
        S.op("dve", lambda e: e.tensor_tensor(out=msq, in0=mean, in1=mean, op=ALU.mult), reads=[sb_], writes=[sb_])
        S.op("dve", lambda e: e.scalar_tensor_tensor(out=rstd, in0=S2, scalar=1.0 / D, in1=msq,
                                                     op0=ALU.mult, op1=ALU.subtract), reads=[sb_], writes=[sb_])
        S.op("act", lambda e: e.activation(out=rstd, in_=rstd, func=AF.Sqrt, bias=self.epsc[:, 0:1]),
             reads=[sb_, self.constb], writes=[sb_])
        S.op("dve", lambda e: e.reciprocal(out=rstd, in_=rstd), reads=[sb_], writes=[sb_])
        for tt in range(8):
            for q4 in range(4):
                vs = vt[:, tt, q4 * 512:(q4 + 1) * 512]
                S.op("dve", lambda e, i=vs, a=mean[:, tt:tt + 1], b=rstd[:, tt:tt + 1]:
                     e.tensor_scalar(out=tmp, in0=i, scalar1=a, scalar2=b, op0=ALU.subtract, op1=ALU.mult),
                     reads=[vtb[tt], sb_], writes=[tmpb], extra=bar)
                S.op("dve", lambda e, g=grep_[:, q4 * 512:(q4 + 1) * 512]:
                     e.tensor_tensor(out=tmp, in0=tmp, in1=g, op=ALU.mult), reads=[tmpb, gbb], writes=[tmpb])
                S.op("dve", lambda e, o=vs, b=brep[:, q4 * 512:(q4 + 1) * 512]:
                     e.tensor_tensor(out=o, in0=tmp, in1=b, op=ALU.add), reads=[tmpb, gbb], writes=[vtb[tt]])
        for uu in range(8):
            Wv, Wb = self.slab_in("c_w_in", oi, uu * 256)
            for ii in range(2):
                for half in range(2):
                    pb, pbb = self.psalloc()
                    for kc in range(NCH):
                        self.mm(pb[:, :], Wv[:, kc, ii * 128:(ii + 1) * 128],
                                self.hT[:, kc, half * 512:(half + 1) * 512], kc == 0, kc == NCH - 1,
                                reads=[Wb, self.hb[kc]], writes=[pbb])
                    S.op("act", lambda e, o=ub[:, ii, half * 512:(half + 1) * 512], i=pb[:, :]:
                         e.activation(out=o, in_=i, func=AF.Gelu), reads=[pbb], writes=[ubb[ii], stgb], extra=bar)
            for ii in range(2):
                chn = uu * 2 + ii
                for half in range(2):
                    pb, pbb = self.psalloc()
                    for t4 in range(4):
                        tt = half * 4 + t4
                        o = pb[:, t4 * 128:(t4 + 1) * 128]
                        self.mm(o, vt[:, tt, chn * 128:(chn + 1) * 128], wmT[:, uu, :], True, False,
                                reads=[vtb[tt], wmb], writes=[pbb])
                        self.mm(o, self.ones[0:1, :], bsh[0:1, uu * 128:(uu + 1) * 128], False, False,
                                reads=[self.constb, bsb], writes=[pbb])
                        self.mm(o, self.ones[0:1, :], bsl[0:1, uu * 128:(uu + 1) * 128], False, True,
                                reads=[self.constb, bsb], writes=[pbb])
                    S.op("dve", lambda e, o=ub[:, ii, half * 512:(half + 1) * 512], i=pb[:, :]:
                         e.tensor_tensor(out=o, in0=i, in1=o, op=ALU.mult),
                         reads=[pbb, ubb[ii]], writes=[ubb[ii]])
            Ov, Ob = self.slab_rows("c_w_out", oi, uu * 256)
            self.proj_out(Ov, Ob, 2, [ub[:, 0, :], ub[:, 1, :]], ubb)
        A.release(m)

    def build(self):
        nc = self.nc
        S = self.S
        self.declare()
        with contextlib.ExitStack() as es:
            def sb(name, shape, dt):
                return es.enter_context(nc.sbuf_tensor("sb_" + name, shape, dt))

            self.xT = sb("xT", [P, NCH, T], F32)
            self.hT = sb("hT", [P, NCH, T], BF16)
            self.ring32 = [sb(f"ring{i}", [P, SLAB], F32) for i in range(NSLOT)]
            self.ring16 = [r[:, :].bitcast(BF16) for r in self.ring32]
            self.ones = sb("ones", [P, P], BF16)
            self.gmix = sb("gmix", [P, self.nl * NCH], F32)
            self.gffn = sb("gffn", [P, self.nl * NCH], F32)
            self.gfin = sb("gfin", [P, NCH], F32)
            self.flags = sb("flags", [P, 2], F32)
            self.epsc = sb("epsc", [P, 1], F32)
            self.zeroc = sb("zeroc", [P, 1], F32)
            self.convw = sb("convw", [P, max(1, self.ne) * 24], F32)
            ARENA_BYTES = 62 * 1024
            arena_t = sb("arena", [P, ARENA_BYTES // 2], BF16)
            self.arena = Arena(arena_t, ARENA_BYTES)
            self.ps = [es.enter_context(nc.psum_tensor(f"ps{i}", [P, 512], F32)) for i in range(8)]
            self.psb = [Buf() for _ in range(8)]
            self.ringb = [Buf() for _ in range(NSLOT)]
            self.xb = [[Buf(), Buf()] for _ in range(NCH)]
            self.hb = [Buf() for _ in range(NCH)]
            self.constb = Buf()

            xv = self.d_x.rearrange("(c p) t -> p c t", p=P)
            for k in range(4):
                S.dma("sp", f"xin{k}", lambda e, k=k: e.dma_start(out=self.xT[:, 4 * k:4 * k + 4, :],
                                                                   in_=xv[:, 4 * k:4 * k + 4, :]),
                      writes=[b for c in range(4 * k, 4 * k + 4) for b in self.xb[c]])
            cl = [("cg0", self.gmix, self.d_gmix), ("cg1", self.gffn, self.d_gffn), ("cg2", self.gfin, self.d_gfin),
                  ("cg3", self.flags, self.d_flag)]
            if self.ne:
                cl.append(("cg4", self.convw, self.d_cw))
            ctoks = []
            for key, dst, src in cl:
                ctoks.append(S.dma("act", key, lambda e, o=dst, i=src: e.dma_start(out=o[:, :], in_=i[:, :])))
            ctoks.append(S.op("dve", lambda e: e.memset(self.ones[:, :], 1.0)))
            ctoks.append(S.op("dve", lambda e: e.memset(self.epsc[:, :], EPS)))
            ctoks.append(S.op("dve", lambda e: e.memset(self.zeroc[:, :], 0.0)))
            for t in ctoks:
                self.constb.w[t[0]] = max(self.constb.w.get(t[0], 0), t[1])

            outs = []
            ei = oi = 0
            import os
            dbg = os.environ.get("K_DBG", "")
            for li, l in enumerate(self.layers):
                if "nomix" not in dbg:
                    self.rmsnorm(self.gmix[:, li * NCH:(li + 1) * NCH])
                    if l % 2 == 0:
                        self.mixer_even(ei)
                        ei += 1
                    else:
                        self.mixer_odd(oi)
                        oi += 1
                if "noffn" not in dbg:
                    self.rmsnorm(self.gffn[:, li * NCH:(li + 1) * NCH])
                    self.ffn(li)
            if self.last:
                outs = self.rmsnorm(self.gfin, final=True)
            else:
                yv = self.d_out.rearrange("(c p) t -> p c t", p=P)
                for k in range(4):
                    outs.append(S.dma("sp", f"out{k}", lambda e, k=k: e.dma_start(
                        out=yv[:, 4 * k:4 * k + 4, :], in_=self.xT[:, 4 * k:4 * k + 4, :]),
                        reads=[b for c in range(4 * k, 4 * k + 4) for b in self.xb[c]]))
            S.wait_only("sp", outs)

            sems = {k: es.enter_context(nc.semaphore(f"s_{k}")) for k in sorted(S.keys)}
            block = es.enter_context(nc.Block())

            @block.tensor
            def _(e):
                S.replay("pe", e, sems)

            @block.scalar
            def _(e):
                S.replay("act", e, sems)

            @block.vector
            def _(e):
                S.replay("dve", e, sems)

            @block.gpsimd
            def _(e):
                S.replay("pool", e, sems)

            @block.sync
            def _(e):
                S.replay("sp", e, sems)
        return nc


def _bias_tables(rel_bias):
    k = np.arange(P)[:, None]
    q = np.arange(640)[None, :]
    idx = np.clip(q - k, -256, 256) + 256
    g = rel_bias[:, :, idx]
    ne = rel_bias.shape[0]
    g = g.reshape(ne, 4, 2, P, 640).transpose(0, 1, 3, 2, 4)
    return np.ascontiguousarray(g.reshape(ne * 4, P, 2 * 640)).astype(np.float32)


def _mask_neg():
    k = np.arange(P)[:, None] // 64
    q = np.arange(640)[None, :] // 64
    valid = (q - k >= 0) & (q - k <= 8)
    return np.where(valid, 0.0, NEG).astype(np.float32)


def _pm(v):
    v = np.asarray(v, dtype=np.float32).reshape(-1, NCH, P)
    return np.ascontiguousarray(v.transpose(2, 0, 1).reshape(P, -1))


_NC_CACHE = {}


def _run_segment(layers, first, last, xT_shards, inp):
    key = (tuple(layers), first, last)
    if key not in _NC_CACHE:
        planner = Builder(layers, first, last)
        planner.build()
        _NC_CACHE[key] = Builder(layers, first, last, plan=planner.slab_log).build()
    nc = _NC_CACHE[key]
    L = list(layers)
    ev = [l // 2 for l in L if l % 2 == 0]
    od = [l // 2 for l in L if l % 2 == 1]
    shared = {
        "g_mix": _pm(inp["mix_norm"][L]),
        "g_ffn": _pm(inp["ffn_norm"][L]),
        "g_fin": _pm(inp["final_norm"][None, :]),
    }
    import os
    if "noffn" not in os.environ.get("K_DBG", ""):
        shared["w_gate"] = np.ascontiguousarray(inp["ffn_w_gate"][L])
        shared["w_up"] = np.ascontiguousarray(inp["ffn_w_up"][L])
        shared["w_down"] = np.ascontiguousarray(inp["ffn_w_down"][L])
    if ev:
        shared["ab_w_in"] = np.ascontiguousarray(inp["ab_w_in"][ev])
        shared["ab_w_out"] = np.ascontiguousarray(inp["ab_w_out"][ev])
        shared["ab_biasg"] = _bias_tables(np.asarray(inp["ab_rel_bias"])[ev])
        shared["ab_maskneg"] = _mask_neg()
        cw = np.asarray(inp["ab_conv_w"])[ev].reshape(len(ev), 3, 8, P)
        shared["ab_convw"] = np.ascontiguousarray(cw.transpose(3, 0, 1, 2).reshape(P, len(ev) * 24))
    if od:
        shared["c_w_in"] = np.ascontiguousarray(inp["c_w_in"][od])
        shared["c_w_out"] = np.ascontiguousarray(inp["c_w_out"][od])
        shared["c_ln_g_rep"] = np.ascontiguousarray(np.broadcast_to(inp["c_ln_g"][od][:, None, :], (len(od), P, D)))
        shared["c_ln_b_rep"] = np.ascontiguousarray(np.broadcast_to(inp["c_ln_b"][od][:, None, :], (len(od), P, D)))
        shared["c_w_sT"] = np.ascontiguousarray(np.asarray(inp["c_w_s"])[od].transpose(0, 1, 3, 2))
        shared["c_b_s"] = np.ascontiguousarray(np.asarray(inp["c_b_s"])[od].reshape(len(od), 1, 1024))
    in_maps = []
    for c in range(N_CORES):
        m = dict(shared)
        m["xT_in"] = xT_shards[c]
        fl = np.zeros((P, 2), np.float32)
        if c % 2 == 1:
            fl[:, 0] = 1.0
            fl[:, 1] = 0.0
        else:
            fl[:, 0] = 0.0
            fl[:, 1] = NEG
        m["flags"] = fl
        in_maps.append(m)
    res = run_bass_kernel_spmd(nc, in_maps, core_ids=list(range(N_CORES)))
    return [np.asarray(r["yT_out"]) for r in res.results]


SEGMENTS = [[0, 1, 2, 3]]


def kernel(**inputs):
    inp = {k: np.asarray(v) for k, v in inputs.items()}
    x = inp["x"].astype(np.float32, copy=False)
    B, Sq, _ = x.shape
    shards = []
    for c in range(N_CORES):
        b, hf = c // 2, c % 2
        shards.append(np.ascontiguousarray(x[b, hf * T:(hf + 1) * T, :].T))
    for si, seg in enumerate(SEGMENTS):
        shards = _run_segment(seg, si == 0, si == len(SEGMENTS) - 1, shards, inp)
    out = np.empty((B, Sq, D), np.float32)
    for c in range(N_CORES):
        b, hf = c // 2, c % 2
        out[b, hf * T:(hf + 1) * T, :] = shards[c].T
    return out
```

```python
import contextlib
import numpy as np
import concourse.bass as bass
import concourse.mybir as mybir
from concourse.bass_utils import run_bass_kernel_spmd

F32 = mybir.dt.float32
BF16 = mybir.dt.bfloat16
AF = mybir.ActivationFunctionType
ALU = mybir.AluOpType
AX = mybir.AxisListType

P = 128
D = 2048
NCH = 16
T = 1024
HALO = 512
FF = 5632
NHC = 44
DEPTH = 4
EPS = 1e-6
NSLOT = 3
SLAB = 4096
NEG = -30000.0
SCALE = 128 ** -0.5
N_CORES = 8
import os as _os
NROT = int(_os.environ.get('K_NROT', '6'))
MM_PAT = _os.environ.get('K_PAT', 'C')


class Buf:
    __slots__ = ("w", "r")

    def __init__(self):
        self.w = {}
        self.r = {}


class Sched:
    ENGS = ("pe", "act", "dve", "pool", "sp")

    def __init__(self):
        self.q = {e: [] for e in self.ENGS}
        self.cnt = {}
        self.seen = {e: {} for e in self.ENGS}
        self.keys = set(self.ENGS)

    def _deps(self, eng, reads, writes, extra):
        need = {}

        def add(d):
            for k, v in d.items():
                if v > need.get(k, 0):
                    need[k] = v

        for b in reads:
            add(b.w)
        for b in writes:
            add(b.w)
            add(b.r)
        for t in extra:
            if t is not None:
                add({t[0]: t[1]})
        waits = []
        seen = self.seen[eng]
        for k, v in need.items():
            if seen.get(k, 0) < v:
                waits.append((k, v))
                seen[k] = v
        return waits

    def raw(self, queue, key, inc, fn, reads=(), writes=(), extra=()):
        waits = self._deps(queue, reads, writes, extra)
        self.keys.add(key)
        self.cnt[key] = self.cnt.get(key, 0) + inc
        tok = (key, self.cnt[key])
        self.q[queue].append((waits, fn, key, inc))
        for b in reads:
            if tok[1] > b.r.get(key, 0):
                b.r[key] = tok[1]
        for b in writes:
            b.w = {key: tok[1]}
            b.r = {}
        return tok

    def op(self, eng, fn, reads=(), writes=(), extra=()):
        return self.raw(eng, eng, 1, fn, reads, writes, extra)

    def dma(self, queue, key, fn, reads=(), writes=(), extra=()):
        return self.raw(queue, key, 16, fn, reads, writes, extra)

    def wait_only(self, eng, toks):
        waits = self._deps(eng, (), (), toks)
        self.q[eng].append((waits, None, None, 0))

    def replay(self, eng, e, sems):
        for waits, fn, key, inc in self.q[eng]:
            for (k, v) in waits:
                e.wait_ge(sems[k], v)
            if fn is not None:
                fn(e).then_inc(sems[key], inc)


class Arena:
    def __init__(self, ap, nbytes):
        self.ap = ap
        self.n = nbytes
        self.off = 0

    def mark(self):
        return self.off

    def release(self, m):
        self.off = m

    def alloc(self, nbytes, dtype, shape=None):
        nbytes = (nbytes + 31) // 32 * 32
        o = self.off
        if o + nbytes > self.n:
            raise RuntimeError(f"arena overflow: {o}+{nbytes} > {self.n}")
        self.off = o + nbytes
        v = self.ap[:, o // 2:(o + nbytes) // 2]
        if dtype == F32:
            v = v.bitcast(F32)
        if shape is not None:
            if len(shape) == 2:
                v = v.rearrange("p (a b) -> p a b", b=shape[1])
            elif len(shape) == 3:
                v = v.rearrange("p (a b c) -> p a b c", b=shape[1], c=shape[2])
        return v


class Builder:
    def __init__(self, layers, first, last, plan=None):
        self.layers = list(layers)
        self.first = first
        self.last = last
        self.S = Sched()
        self.nc = bass.Bass("TRN2", target_bir_lowering=False)
        self.slab_i = 0
        self.ps_i = 0
        self.barrier = None
        self.last_x = None
        self.nrot = 8
        self.plan = plan
        self.slab_log = []
        self.dma_issued = 0
        self.cast_issued = 0
        self.dram = {}

    def declare(self):
        nc = self.nc
        L = self.layers
        ne = len([l for l in L if l % 2 == 0])
        no = len([l for l in L if l % 2 == 1])
        nl = len(L)
        self.ne, self.no, self.nl = ne, no, nl

        def din(name, shape, dt=F32):
            ap = nc.dram_tensor(name, list(shape), dt, kind="ExternalInput").ap()
            self.dram[name] = ap
            return ap

        self.d_x = din("xT_in", [D, T])
        self.d_out = nc.dram_tensor("yT_out", [D, T], F32, kind="ExternalOutput").ap()
        self.d_gmix = din("g_mix", [P, nl * NCH])
        self.d_gffn = din("g_ffn", [P, nl * NCH])
        self.d_gfin = din("g_fin", [P, NCH])
        self.d_flag = din("flags", [P, 2])
        import os
        self.dbg = os.environ.get("K_DBG", "")
        if "noffn" not in self.dbg:
            self.d_wg = din("w_gate", [nl, D, FF])
            self.d_wu = din("w_up", [nl, D, FF])
            self.d_wd = din("w_down", [nl, FF, D])
        if ne:
            self.d_win = din("ab_w_in", [ne, D, 6144])
            self.d_wout = din("ab_w_out", [ne, D, D])
            self.d_bias = din("ab_biasg", [ne * 4, P, 2 * 640])
            self.d_mask = din("ab_maskneg", [P, 640])
            self.d_cw = din("ab_convw", [P, ne * 24])
            self.d_hin = [nc.dram_tensor(f"hin{i}", [D, HALO], BF16) for i in range(ne)]
            self.d_hout = [nc.dram_tensor(f"hout{i}", [2 * D, HALO], BF16) for i in range(ne)]
        if no:
            self.d_cwin = din("c_w_in", [no, D, 4096])
            self.d_cwout = din("c_w_out", [no, D, D])
            self.d_lng = din("c_ln_g_rep", [no, P, D])
            self.d_lnb = din("c_ln_b_rep", [no, P, D])
            self.d_wsT = din("c_w_sT", [no, 8, P, P])
            self.d_bs = din("c_b_s", [no, 1, 1024])

    def psalloc(self):
        i = self.ps_i % self.nrot
        self.ps_i += 1
        return self.ps[i], self.psb[i]

    def _slab_views(self, desc, s):
        kind, name, idx, off = desc
        w = self.dram[name][idx]
        if kind == "in":
            src = w.rearrange("(c p) n -> p c n", p=P)[:, :, off:off + 256]
            dst = self.ring32[s][:, :].rearrange("p (c n) -> p c n", n=256)
            view = self.ring16[s][:, 0:SLAB].rearrange("p (c n) -> p c n", n=256)
        else:
            src = w[off:off + 256, :].rearrange("(k p) n -> p k n", p=P)
            dst = self.ring32[s][:, :].rearrange("p (k n) -> p k n", n=2048)
            view = self.ring16[s][:, 0:SLAB].rearrange("p (k n) -> p k n", n=2048)
        return src, dst, view

    def _issue_dma(self, k, desc):
        s = k % NSLOT
        src, dst, _ = self._slab_views(desc, s)
        self.S.dma("sp", f"slot{s}", lambda e, o=dst, i=src: e.dma_start(out=o, in_=i), writes=[self.ringb[s]])

    def _issue_cast(self, k):
        s = k % NSLOT
        o = self.ring16[s][:, 0:SLAB]
        i = self.ring32[s][:, :]
        self.S.op("act", lambda e, o=o, i=i: e.activation(out=o, in_=i, func=AF.Copy),
                  reads=[self.ringb[s]], writes=[self.ringb[s]])

    def _get_slab(self, desc, keep_prev=0):
        k = self.slab_i
        self.slab_i += 1
        s = k % NSLOT
        if self.plan is None:
            self.slab_log.append(desc)
            self._issue_dma(k, desc)
            self._issue_cast(k)
        else:
            assert self.plan[k] == desc, (k, self.plan[k], desc)
            n = len(self.plan)
            while self.dma_issued < min(k - keep_prev + NSLOT, n):
                self._issue_dma(self.dma_issued, self.plan[self.dma_issued])
                self.dma_issued += 1
            while self.cast_issued < min(k + 2, n):
                self._issue_cast(self.cast_issued)
                self.cast_issued += 1
            if k > 0 and self.S.cnt.get("pe", 0) > 0:
                self.S.wait_only("pool", [("pe", self.S.cnt["pe"])])
        return self._slab_views(desc, s)[2], self.ringb[s]

    def slab_in(self, wname, idx, c0):
        return self._get_slab(("in", wname, idx, c0))

    def slab_rows(self, wname, idx, r0, keep_prev=0):
        return self._get_slab(("rows", wname, idx, r0), keep_prev)

    def mm_order(self):
        if MM_PAT == "A":
            return [(i, kc, half) for i in range(2) for kc in range(NCH) for half in range(2)]
        if MM_PAT == "B":
            return [(i, kc, half) for i in range(2) for half in range(2) for kc in range(NCH)]
        if MM_PAT == "C":
            return [(i, kc, half) for kc in range(NCH) for half in range(2) for i in range(2)]
        return [(i, kc, half) for kc in range(NCH) for i in range(2) for half in range(2)]

    def mm(self, out, lhsT, rhs, start, stop, reads, writes):
        return self.S.op("pe", lambda e: e.matmul(out, lhsT=lhsT, rhs=rhs, start=start, stop=stop),
                         reads=reads, writes=writes)

    def xadd(self, dc, half, pb, pbb):
        xs = self.xT[:, dc, half * 512:(half + 1) * 512]
        self.last_x = self.S.op("dve", lambda e: e.tensor_tensor(out=xs, in0=xs, in1=pb[:, :], op=ALU.add),
                                reads=[pbb], writes=[self.xb[dc][half]])
        return self.last_x

    def proj_out(self, Wv, Wb, nk, acts, actb):
        tok = None
        for dcp in range(0, NCH, 2):
            accs = [(dc, half, self.psalloc()) for dc in (dcp, dcp + 1) for half in range(2)]
            for k in range(nk):
                for (dc, half, (pb, pbb)) in accs:
                    self.mm(pb[:, :], Wv[:, k, dc * 128:(dc + 1) * 128],
                            acts[k][:, half * 512:(half + 1) * 512], k == 0, k == nk - 1,
                            reads=[Wb, actb[k]], writes=[pbb])
            for (dc, half, (pb, pbb)) in accs:
                tok = self.xadd(dc, half, pb, pbb)
        return tok

    def rmsnorm(self, gain, final=False):
        S = self.S
        A = self.arena
        m = A.mark()
        nb = [self.last_x]
        self.sq = A.alloc(2 * T * 2, BF16, (2, T))
        self.rs = A.alloc(T * 4, F32)
        self.sqb = [Buf(), Buf()]
        self.rsb = [Buf(), Buf()]
        if final:
            self.obuf = A.alloc(2 * T * 4, F32, (2, T))
            self.obb = [Buf(), Buf()]
        pa = [self.psalloc(), self.psalloc()]
        for c in range(NCH):
            sq = self.sq[:, c % 2, :]
            sqb = self.sqb[c % 2]
            S.op("act", lambda e, o=sq, i=self.xT[:, c, :]: e.activation(out=o, in_=i, func=AF.Square),
                 reads=self.xb[c], writes=[sqb], extra=nb)
            for half in range(2):
                self.mm(pa[half][0][:, :], self.ones[:, :], sq[:, half * 512:(half + 1) * 512],
                        c == 0, c == NCH - 1, reads=[sqb, self.constb], writes=[pa[half][1]])
        for half in range(2):
            rs = self.rs[:, half * 512:(half + 1) * 512]
            S.op("act", lambda e, o=rs, i=pa[half][0][:, :]: e.activation(
                out=o, in_=i, func=AF.Sqrt, bias=self.epsc[:, 0:1], scale=1.0 / D),
                reads=[pa[half][1], self.constb], writes=[self.rsb[half]])
            S.op("dve", lambda e, o=rs: e.reciprocal(out=o, in_=o),
                 reads=[self.rsb[half]], writes=[self.rsb[half]])
        tok = None
        outs = []
        for c in range(NCH):
            if not final:
                o = self.hT[:, c, :]
                tok = S.op("dve", lambda e, o=o, i=self.xT[:, c, :], g=gain[:, c:c + 1]:
                           e.scalar_tensor_tensor(out=o, in0=i, scalar=g, in1=self.rs[:, :],
                                                  op0=ALU.mult, op1=ALU.mult),
                           reads=self.xb[c] + self.rsb + [self.constb], writes=[self.hb[c]])
            else:
                ob = self.obuf[:, c % 2, :]
                S.op("dve", lambda e, o=ob, i=self.xT[:, c, :], g=gain[:, c:c + 1]:
                     e.scalar_tensor_tensor(out=o, in0=i, scalar=g, in1=self.rs[:, :],
                                            op0=ALU.mult, op1=ALU.mult),
                     reads=self.xb[c] + self.rsb + [self.constb], writes=[self.obb[c % 2]], extra=nb)
                dst = self.d_out[c * 128:(c + 1) * 128, :]
                outs.append(S.dma("sp", f"out{c % 2}", lambda e, o=dst, i=ob: e.dma_start(out=o, in_=i),
                                  reads=[self.obb[c % 2]]))
        self.barrier = tok
        A.release(m)
        return outs

    def ffn(self, li):
        S = self.S
        A = self.arena
        m = A.mark()
        self.nrot = 8
        act = A.alloc(4 * T * 2, BF16, (4, T))
        actb = [Buf() for _ in range(4)]
        sg = A.alloc(4 * 512 * 4, F32, (4, 512))
        sgb = [Buf() for _ in range(4)]
        for grp in range(NHC // 4):
            for jj in range(2):
                j = grp * 2 + jj
                Gv, Gb = self.slab_in("w_gate", li, j * 256)
                pg = [[self.psalloc(), self.psalloc()] for _ in range(2)]
                for (i, kc, half) in self.mm_order():
                    self.mm(pg[i][half][0][:, :], Gv[:, kc, i * 128:(i + 1) * 128],
                            self.hT[:, kc, half * 512:(half + 1) * 512], kc == 0, kc == NCH - 1,
                            reads=[Gb, self.hb[kc]], writes=[pg[i][half][1]])
                for i in range(2):
                    for half in range(2):
                        S.op("act", lambda e, o=sg[:, i * 2 + half, :], i_=pg[i][half][0][:, :]:
                             e.activation(out=o, in_=i_, func=AF.Silu),
                             reads=[pg[i][half][1]], writes=[sgb[i * 2 + half]], extra=[self.barrier])
                Uv, Ub = self.slab_in("w_up", li, j * 256)
                pu = [[self.psalloc(), self.psalloc()] for _ in range(2)]
                for (i, kc, half) in self.mm_order():
                    self.mm(pu[i][half][0][:, :], Uv[:, kc, i * 128:(i + 1) * 128],
                            self.hT[:, kc, half * 512:(half + 1) * 512], kc == 0, kc == NCH - 1,
                            reads=[Ub, self.hb[kc]], writes=[pu[i][half][1]])
                for i in range(2):
                    hc = jj * 2 + i
                    for half in range(2):
                        S.op("dve", lambda e, o=act[:, hc, half * 512:(half + 1) * 512], a=sg[:, i * 2 + half, :],
                             b=pu[i][half][0][:, :]: e.tensor_tensor(out=o, in0=a, in1=b, op=ALU.mult),
                             reads=[sgb[i * 2 + half], pu[i][half][1]], writes=[actb[hc]])
            D0v, D0b = self.slab_rows("w_down", li, (grp * 4) * 128)
            D1v, D1b = self.slab_rows("w_down", li, (grp * 4 + 2) * 128, keep_prev=1)
            for dcp in range(0, NCH, 2):
                accs = [(dc, half, self.psalloc()) for dc in (dcp, dcp + 1) for half in range(2)]
                for k in range(4):
                    Dv, Db = (D0v, D0b) if k < 2 else (D1v, D1b)
                    for (dc, half, (pb, pbb)) in accs:
                        self.mm(pb[:, :], Dv[:, k % 2, dc * 128:(dc + 1) * 128],
                                act[:, k, half * 512:(half + 1) * 512], k == 0, k == 3,
                                reads=[Db, actb[k]], writes=[pbb])
                for (dc, half, (pb, pbb)) in accs:
                    self.xadd(dc, half, pb, pbb)
        A.release(m)

    def mixer_even(self, ei):
        S = self.S
        nc = self.nc
        A = self.arena
        m = A.mark()
        bar = [self.barrier]
        hh_t = A.alloc(NCH * HALO * 2, BF16, (NCH, HALO))
        hhb = Buf()
        hin_v = self.d_hin[ei].ap().rearrange("(c p) n -> p c n", p=P)
        hout_v = self.d_hout[ei].ap().rearrange("(c p) n -> p c n", p=P)
        tw = S.dma("sp", "hw", lambda e: e.dma_start(out=hin_v, in_=self.hT[:, :, T - HALO:T]),
                   reads=self.hb)
        hin_t, hout_t = self.d_hin[ei], self.d_hout[ei]
        tc = S.raw("pool", f"cc{ei}", 1, lambda e: e.collective_compute(
            "AllGather", ALU.bypass, replica_groups=[[0, 1], [2, 3], [4, 5], [6, 7]],
            ins=[hin_t.ap().opt()], outs=[hout_t.ap().opt()]), extra=[tw])
        S.dma("sp", "hr", lambda e: e.dma_start(out=hh_t, in_=hout_v[:, 0:NCH, :]),
              writes=[hhb], extra=[tc] + bar)
        mask = A.alloc(640 * 4, F32)
        maskb = Buf()
        S.dma("sp", "cm", lambda e: e.dma_start(out=mask, in_=self.d_mask[:, :]), writes=[maskb], extra=bar)
        m2 = A.mark()
        qT = A.alloc(2 * T * 2, BF16, (2, T))
        kT = A.alloc(2 * (T + HALO) * 2, BF16, (2, T + HALO))
        Vt = A.alloc(12 * 256 * 2, BF16, (12, 256))
        tb = A.alloc(2 * 640 * 4, F32, (2, 640))
        tmp = A.alloc(2 * 640 * 4, F32, (2, 640))
        Pb = A.alloc(6 * 640 * 2, BF16, (6, 640))
        rinv = A.alloc(512 * 4, F32)
        ao = A.alloc(2 * T * 2, BF16, (2, T))
        qb = [Buf(), Buf()]
        kb = [Buf(), Buf()]
        Vb = [Buf() for _ in range(12)]
        tbb = Buf()
        tmpb = [Buf(), Buf()]
        Pbb = [Buf() for _ in range(6)]
        rinvb = Buf()
        aob = [Buf(), Buf()]
        ntmp = 0
        last = None
        for g in range(0 if "noattn" not in self.dbg else 4, 4):
            self.nrot = 6
            Qv, Qb = self.slab_in("ab_w_in", ei, g * 256)
            S.dma("sp", "tb", lambda e, g=g: e.dma_start(out=tb, in_=self.d_bias[ei * 4 + g].rearrange(
                "p (a b) -> p a b", b=640)), writes=[tbb], extra=bar + [last])
            for hh in range(2):
                S.op("dve", lambda e, o=tb[:, hh, :]: e.tensor_tensor(out=o, in0=o, in1=mask, op=ALU.add),
                     reads=[maskb, tbb], writes=[tbb])
            accs = [(hh, half, self.psalloc()) for half in range(2) for hh in range(2)]
            for kc in range(NCH):
                for (hh, half, (pb, pbb)) in accs:
                    self.mm(pb[:, :], Qv[:, kc, hh * 128:(hh + 1) * 128],
                            self.hT[:, kc, half * 512:(half + 1) * 512], kc == 0, kc == NCH - 1,
                            reads=[Qb, self.hb[kc]], writes=[pbb])
            for (hh, half, (pb, pbb)) in accs:
                S.op("act", lambda e, o=qT[:, hh, half * 512:(half + 1) * 512], i=pb[:, :]:
                     e.activation(out=o, in_=i, func=AF.Copy),
                     reads=[pbb], writes=[qb[hh]], extra=bar + [last])
            Kv, Kb = self.slab_in("ab_w_in", ei, 1024 + g * 256)
            for segs in ((1, 2), (0,)):
                accs = [(hh, seg, self.psalloc()) for seg in segs for hh in range(2)]
                for kc in range(NCH):
                    for (hh, seg, (pb, pbb)) in accs:
                        if seg == 0:
                            rhs, rb = hh_t[:, kc, :], hhb
                        else:
                            rhs, rb = self.hT[:, kc, (seg - 1) * 512:seg * 512], self.hb[kc]
                        self.mm(pb[:, :], Kv[:, kc, hh * 128:(hh + 1) * 128], rhs, kc == 0, kc == NCH - 1,
                                reads=[Kb, rb], writes=[pbb])
                for (hh, seg, (pb, pbb)) in accs:
                    S.op("act", lambda e, o=kT[:, hh, seg * 512:(seg + 1) * 512], i=pb[:, :]:
                         e.activation(out=o, in_=i, func=AF.Copy),
                         reads=[pbb], writes=[kb[hh]], extra=bar + [last])
            Vv, Vvb = self.slab_in("ab_w_in", ei, 2048 + g * 256)
            for tq in (1, 2, 0):
                accs = [(tq * 4 + t4, self.psalloc()) for t4 in range(4)]
                for kc in range(NCH):
                    for (tt, (pb, pbb)) in accs:
                        if tt < 4:
                            lh, lb = hh_t[:, kc, tt * 128:(tt + 1) * 128], hhb
                        else:
                            lh, lb = self.hT[:, kc, (tt - 4) * 128:(tt - 3) * 128], self.hb[kc]
                        self.mm(pb[:, 0:256], lh, Vv[:, kc, :], kc == 0, kc == NCH - 1,
                                reads=[Vvb, lb], writes=[pbb])
                for (tt, (pb, pbb)) in accs:
                    S.op("dve", lambda e, o=Vt[:, tt, :], i=pb[:, 0:256]: e.tensor_copy(out=o, in_=i),
                         reads=[pbb], writes=[Vb[tt]], extra=bar + [last])
            for hh in range(2):
                pv = rsb_ = None
                for j in range(12 if "noscore" not in self.dbg else 0):
                    qc0 = max(0, 2 * j - 8)
                    qc1 = min(16, 2 * j + 2)
                    nq = (qc1 - qc0) * 64
                    q0 = qc0 * 64
                    c0 = (qc0 + 8 - 2 * j) * 64
                    tsel = ntmp % 2
                    ntmp += 1
                    for (a0, a1) in ((0, min(nq, 512)), (512, nq)):
                        if a1 <= a0:
                            continue
                        pb, pbb = self.psalloc()
                        self.mm(pb[:, 0:a1 - a0], kT[:, hh, j * 128:(j + 1) * 128],
                                qT[:, hh, q0 + a0:q0 + a1], True, True,
                                reads=[kb[hh], qb[hh]], writes=[pbb])
                        S.op("dve", lambda e, o=tmp[:, tsel, a0:a1], i=pb[:, 0:a1 - a0],
                             t=tb[:, hh, c0 + a0:c0 + a1]: e.scalar_tensor_tensor(
                                 out=o, in0=i, scalar=SCALE, in1=t, op0=ALU.mult, op1=ALU.add),
                             reads=[pbb, tbb], writes=[tmpb[tsel]])
                    bias_ap = self.flags[:, 1:2] if j < 4 else self.zeroc[:, 0:1]
                    S.op("act", lambda e, o=Pb[:, j % 6, 0:nq], i=tmp[:, tsel, 0:nq], b=bias_ap:
                         e.activation(out=o, in_=i, func=AF.Exp, bias=b),
                         reads=[tmpb[tsel], self.constb], writes=[Pbb[j % 6]])
                    if j >= 4 and "nopv" not in self.dbg:
                        p = j - 4
                        slot = p % 4
                        if slot == 0:
                            pv = (self.ps[6], self.psb[6])
                            rsb_ = (self.ps[7], self.psb[7])
                        for jj in range(p, p + 5):
                            col = (2 * p - max(0, 2 * jj - 8)) * 64
                            rhs = Pb[:, jj % 6, col:col + 128]
                            self.mm(pv[0][:, slot * 128:(slot + 1) * 128], Vt[:, jj, hh * 128:(hh + 1) * 128],
                                    rhs, jj == p, jj == p + 4, reads=[Vb[jj], Pbb[jj % 6]], writes=[pv[1]])
                            self.mm(rsb_[0][:, slot * 128:(slot + 1) * 128], self.ones[:, :],
                                    rhs, jj == p, jj == p + 4, reads=[self.constb, Pbb[jj % 6]],
                                    writes=[rsb_[1]])
                        if slot == 3:
                            rnd = p // 4
                            S.op("dve", lambda e, i=rsb_[0][:, :]: e.reciprocal(out=rinv, in_=i),
                                 reads=[rsb_[1]], writes=[rinvb])
                            S.op("dve", lambda e, o=ao[:, hh, rnd * 512:(rnd + 1) * 512], i=pv[0][:, :]:
                                 e.tensor_tensor(out=o, in0=i, in1=rinv, op=ALU.mult),
                                 reads=[pv[1], rinvb], writes=[aob[hh]], extra=bar + [last])
            Ov, Ob = self.slab_rows("ab_w_out", ei, g * 256)
            last = self.proj_out(Ov, Ob, 2, [ao[:, 0, :], ao[:, 1, :]], aob)
        A.release(m2)
        self.nrot = 8
        TZ = T + 8
        zz = A.alloc(2 * TZ * 4, F32, (2, TZ))
        y = A.alloc(2 * T * 4, F32, (2, T))
        co = A.alloc(2 * T * 2, BF16, (2, T))
        zb = [Buf(), Buf()]
        yb = [Buf(), Buf()]
        cob = [Buf(), Buf()]
        cbar = bar + [last]
        cw = self.convw[:, ei * 24:(ei + 1) * 24]
        for cc in range(0 if "noconv" not in self.dbg else 4, 4):
            Cv, Cb = self.slab_in("ab_w_in", ei, 4096 + cc * 256)
            pcs = [[self.psalloc(), self.psalloc()] for _ in range(2)]
            for kc in range(NCH):
                for half in range(2):
                    for ii in range(2):
                        self.mm(pcs[ii][half][0][:, :], Cv[:, kc, ii * 128:(ii + 1) * 128],
                                self.hT[:, kc, half * 512:(half + 1) * 512], kc == 0, kc == NCH - 1,
                                reads=[Cb, self.hb[kc]], writes=[pcs[ii][half][1]])
            for ii in range(2):
                pc = pcs[ii]
                ph_ = self.psalloc()
                for kc in range(NCH):
                    self.mm(ph_[0][:, 0:2], Cv[:, kc, ii * 128:(ii + 1) * 128], hh_t[:, kc, HALO - 2:HALO],
                            kc == 0, kc == NCH - 1, reads=[Cb, hhb], writes=[ph_[1]])
                for half in range(2):
                    S.op("act", lambda e, o=zz[:, ii, 2 + half * 512:2 + (half + 1) * 512], i=pc[half][0][:, :]:
                         e.activation(out=o, in_=i, func=AF.Copy),
                         reads=[pc[half][1]], writes=[zb[ii]], extra=cbar)
                S.op("act", lambda e, o=zz[:, ii, 0:2], i=ph_[0][:, 0:2]: e.activation(out=o, in_=i, func=AF.Copy),
                     reads=[ph_[1]], writes=[zb[ii]], extra=cbar)
            Hv, Hb = self.slab_in("ab_w_in", ei, 5120 + cc * 256)
            phs = [[self.psalloc(), self.psalloc()] for _ in range(2)]
            for kc in range(NCH):
                for half in range(2):
                    for ii in range(2):
                        self.mm(phs[ii][half][0][:, :], Hv[:, kc, ii * 128:(ii + 1) * 128],
                                self.hT[:, kc, half * 512:(half + 1) * 512], kc == 0, kc == NCH - 1,
                                reads=[Hb, self.hb[kc]], writes=[phs[ii][half][1]])
            for ii in range(2):
                ch = cc * 2 + ii
                ph = phs[ii]
                ph_ = self.psalloc()
                for kc in range(NCH):
                    self.mm(ph_[0][:, 0:2], Hv[:, kc, ii * 128:(ii + 1) * 128], hh_t[:, kc, HALO - 2:HALO],
                            kc == 0, kc == NCH - 1, reads=[Hb, hhb], writes=[ph_[1]])
                for half in range(2):
                    zs = zz[:, ii, 2 + half * 512:2 + (half + 1) * 512]
                    S.op("dve", lambda e, o=zs, b=ph[half][0][:, :]:
                         e.tensor_tensor(out=o, in0=o, in1=b, op=ALU.mult),
                         reads=[ph[half][1]], writes=[zb[ii]])
                S.op("dve", lambda e, o=zz[:, ii, 0:2], b=ph_[0][:, 0:2]:
                     e.scalar_tensor_tensor(out=o, in0=o, scalar=self.flags[:, 0:1], in1=b,
                                            op0=ALU.mult, op1=ALU.mult),
                     reads=[ph_[1], self.constb], writes=[zb[ii]])
                w0 = cw[:, 0 * 8 + ch:0 * 8 + ch + 1]
                w1 = cw[:, 1 * 8 + ch:1 * 8 + ch + 1]
                w2 = cw[:, 2 * 8 + ch:2 * 8 + ch + 1]
                ys = y[:, ii, :]
                S.op("dve", lambda e, w=w0, o=ys, i=zz[:, ii, 0:T]: e.tensor_scalar(
                    out=o, in0=i, scalar1=w, scalar2=None, op0=ALU.mult),
                    reads=[zb[ii], self.constb], writes=[yb[ii]], extra=cbar)
                S.op("dve", lambda e, w=w1, o=ys, i=zz[:, ii, 1:T + 1]: e.scalar_tensor_tensor(
                    out=o, in0=i, scalar=w, in1=o, op0=ALU.mult, op1=ALU.add),
                    reads=[zb[ii], self.constb], writes=[yb[ii]])
                S.op("dve", lambda e, w=w2, o=ys, i=zz[:, ii, 2:T + 2]: e.scalar_tensor_tensor(
                    out=o, in0=i, scalar=w, in1=o, op0=ALU.mult, op1=ALU.add),
                    reads=[zb[ii], self.constb], writes=[yb[ii]])
            Bv, Bb = self.slab_in("ab_w_in", ei, 3072 + cc * 256)
            pbqs = [[self.psalloc(), self.psalloc()] for _ in range(2)]
            for kc in range(NCH):
                for half in range(2):
                    for ii in range(2):
                        self.mm(pbqs[ii][half][0][:, :], Bv[:, kc, ii * 128:(ii + 1) * 128],
                                self.hT[:, kc, half * 512:(half + 1) * 512], kc == 0, kc == NCH - 1,
                                reads=[Bb, self.hb[kc]], writes=[pbqs[ii][half][1]])
            for ii in range(2):
                pbq = pbqs[ii]
                for half in range(2):
                    S.op("dve", lambda e, o=co[:, ii, half * 512:(half + 1) * 512],
                         a=y[:, ii, half * 512:(half + 1) * 512], b=pbq[half][0][:, :]:
                         e.tensor_tensor(out=o, in0=a, in1=b, op=ALU.mult),
                         reads=[yb[ii], pbq[half][1]], writes=[cob[ii]], extra=cbar)
            Ov, Ob = self.slab_rows("ab_w_out", ei, 1024 + cc * 256)
            self.proj_out(Ov, Ob, 2, [co[:, 0, :], co[:, 1, :]], cob)
        A.release(m)

    def mixer_odd(self, oi):
        S = self.S
        A = self.arena
        self.nrot = 8
        m = A.mark()
        bar = [self.barrier]
        vt = A.alloc(8 * D * 2, BF16, (8, D))
        vtb = [Buf() for _ in range(8)]
        grep_ = A.alloc(D * 4, F32)
        brep = A.alloc(D * 4, F32)
        gbb = Buf()
        S.dma("sp", "lg", lambda e: e.dma_start(out=grep_, in_=self.d_lng[oi]), writes=[gbb], extra=bar)
        S.dma("sp", "lb", lambda e: e.dma_start(out=brep, in_=self.d_lnb[oi]), writes=[gbb], extra=bar)
        wmT = A.alloc(8 * P * 2, BF16, (8, P))
        wmb = Buf()
        tmp = A.alloc(512 * 4, F32)
        tmpb = Buf()
        ub = A.alloc(2 * T * 2, BF16, (2, T))
        ubb = [Buf(), Buf()]
        stg = ub.rearrange("p a b -> p (a b)").bitcast(F32)
        stgb = Buf()
        bsh = A.alloc(1024 * 2, BF16)
        bsl = A.alloc(1024 * 2, BF16)
        bsb = Buf()
        S.dma("sp", "wm", lambda e: e.dma_start(out=stg.rearrange("p (g t) -> p g t", t=P),
                                                in_=self.d_wsT[oi].rearrange("g s t -> s g t")),
              writes=[stgb], extra=bar)
        S.op("dve", lambda e: e.tensor_copy(out=wmT, in_=stg.rearrange("p (g t) -> p g t", t=P)),
             reads=[stgb], writes=[wmb], extra=bar)
        S.op("dve", lambda e: e.memset(wmT[64:128, :, 0:64], 0.0), writes=[wmb])
        bsf = stg
        S.dma("sp", "bs", lambda e: e.dma_start(out=bsf[0:1, :], in_=self.d_bs[oi]), writes=[stgb], extra=bar)
        S.op("dve", lambda e: e.tensor_copy(out=bsh[0:1, :], in_=bsf[0:1, :]), reads=[stgb], writes=[bsb], extra=bar)
        S.op("dve", lambda e: e.tensor_tensor(out=bsf[0:1, :], in0=bsf[0:1, :], in1=bsh[0:1, :], op=ALU.subtract),
             reads=[bsb, stgb], writes=[stgb])
        S.op("dve", lambda e: e.tensor_copy(out=bsl[0:1, :], in_=bsf[0:1, :]), reads=[stgb], writes=[bsb], extra=bar)
        s1 = A.alloc(64 * 4, F32)
        s2 = A.alloc(64 * 4, F32)
        st = A.alloc(5 * 8 * 4, F32, (5, 8))
        sb_ = Buf()
        junk = ub[:, 1, 0:256]
        junkb = stgb
        for vv in range(8):
            Wv, Wb = self.slab_in("c_w_in", oi, 2048 + vv * 256)
            for tq in range(2):
                accs = [(tq * 4 + t4, self.psalloc()) for t4 in range(4)]
                for kc in range(NCH):
                    for (tt, (pb, pbb)) in accs:
                        self.mm(pb[:, 0:256], self.hT[:, kc, tt * 128:(tt + 1) * 128],
                                Wv[:, kc, :], kc == 0, kc == NCH - 1, reads=[Wb, self.hb[kc]], writes=[pbb])
                for (tt, (pb, pbb)) in accs:
                    vs = vt[:, tt, vv * 256:(vv + 1) * 256]
                    S.op("act", lambda e, o=vs, i=pb[:, 0:256], a=s1[:, tt * 8 + vv:tt * 8 + vv + 1]:
                         e.activation(out=o, in_=i, func=AF.Gelu, accum_out=a),
                         reads=[pbb], writes=[vtb[tt], sb_], extra=bar)
                    S.op("act", lambda e, i=vs, a=s2[:, tt * 8 + vv:tt * 8 + vv + 1]:
                         e.activation(out=junk, in_=i, func=AF.Square, accum_out=a),
                         reads=[vtb[tt]], writes=[junkb, sb_], extra=bar)
        S1, S2, mean, rstd, msq = (st[:, k, :] for k in range(5))
        S.op("dve", lambda e: e.tensor_reduce(out=S1, in_=s1.rearrange("p (a b) -> p a b", b=8), axis=AX.X, op=ALU.add),
             reads=[sb_], writes=[sb_])
        S.op("dve", lambda e: e.tensor_reduce(out=S2, in_=s2.rearrange("p (a b) -> p a b", b=8), axis=AX.X, op=ALU.add),
             reads=[sb_], writes=[sb_])
        S.op("dve", lambda e: e.tensor_scalar(out=mean, in0=S1, scalar1=1.0 / D, scalar2=None, op0=ALU.mult),
             reads=[sb_], writes=[sb_])
        S.op("dve", lambda e: e.tensor_tensor(out=msq, in0=mean, in1=mean, op=ALU.mult), reads=[sb_], writes=[sb_])
        S.op("dve", lambda e: e.scalar_tensor_tensor(out=rstd, in0=S2, scalar=1.0 / D, in1=msq,
                                                     op0=ALU.mult, op1=ALU.subtract), reads=[sb_], writes=[sb_])
        S.op("act", lambda e: e.activation(out=rstd, in_=rstd, func=AF.Sqrt, bias=self.epsc[:, 0:1]),
             reads=[sb_, self.constb], writes=[sb_])
        S.op("dve", lambda e: e.reciprocal(out=rstd, in_=rstd), reads=[sb_], writes=[sb_])
        for tt in range(8):
            for q4 in range(4):
                vs = vt[:, tt, q4 * 512:(q4 + 1) * 512]
                S.op("dve", lambda e, i=vs, a=mean[:, tt:tt + 1], b=rstd[:, tt:tt + 1]:
                     e.tensor_scalar(out=tmp, in0=i, scalar1=a, scalar2=b, op0=ALU.subtract, op1=ALU.mult),
                     reads=[vtb[tt], sb_], writes=[tmpb], extra=bar)
                S.op("dve", lambda e, g=grep_[:, q4 * 512:(q4 + 1) * 512]:
                     e.tensor_tensor(out=tmp, in0=tmp, in1=g, op=ALU.mult), reads=[tmpb, gbb], writes=[tmpb])
                S.op("dve", lambda e, o=vs, b=brep[:, q4 * 512:(q4 + 1) * 512]:
                     e.tensor_tensor(out=o, in0=tmp, in1=b, op=ALU.add), reads=[tmpb, gbb], writes=[vtb[tt]])
        for uu in range(8):
            Wv, Wb = self.slab_in("c_w_in", oi, uu * 256)
            accs = [(ii, half, self.psalloc()) for half in range(2) for ii in range(2)]
            for kc in range(NCH):
                for (ii, half, (pb, pbb)) in accs:
                    self.mm(pb[:, :], Wv[:, kc, ii * 128:(ii + 1) * 128],
                            self.hT[:, kc, half * 512:(half + 1) * 512], kc == 0, kc == NCH - 1,
                            reads=[Wb, self.hb[kc]], writes=[pbb])
            for (ii, half, (pb, pbb)) in accs:
                S.op("act", lambda e, o=ub[:, ii, half * 512:(half + 1) * 512], i=pb[:, :]:
                     e.activation(out=o, in_=i, func=AF.Gelu), reads=[pbb], writes=[ubb[ii], stgb], extra=bar)
            for ii in range(2):
                chn = uu * 2 + ii
                for half in range(2):
                    pb, pbb = self.psalloc()
                    for t4 in range(4):
                        tt = half * 4 + t4
                        o = pb[:, t4 * 128:(t4 + 1) * 128]
                        self.mm(o, vt[:, tt, chn * 128:(chn + 1) * 128], wmT[:, uu, :], True, False,
                                reads=[vtb[tt], wmb], writes=[pbb])
                        self.mm(o, self.ones[0:1, :], bsh[0:1, uu * 128:(uu + 1) * 128], False, False,
                                reads=[self.constb, bsb], writes=[pbb])
                        self.mm(o, self.ones[0:1, :], bsl[0:1, uu * 128:(uu + 1) * 128], False, True,
                                reads=[self.constb, bsb], writes=[pbb])
                    S.op("dve", lambda e, o=ub[:, ii, half * 512:(half + 1) * 512], i=pb[:, :]:
                         e.tensor_tensor(out=o, in0=i, in1=o, op=ALU.mult),
                         reads=[pbb, ubb[ii]], writes=[ubb[ii]])
            Ov, Ob = self.slab_rows("c_w_out", oi, uu * 256)
            self.proj_out(Ov, Ob, 2, [ub[:, 0, :], ub[:, 1, :]], ubb)
        A.release(m)

    def build(self):
        nc = self.nc
        S = self.S
        self.declare()
        with contextlib.ExitStack() as es:
            def sb(name, shape, dt):
                return es.enter_context(nc.sbuf_tensor("sb_" + name, shape, dt))

            self.xT = sb("xT", [P, NCH, T], F32)
            self.hT = sb("hT", [P, NCH, T], BF16)
            self.ring32 = [sb(f"ring{i}", [P, SLAB], F32) for i in range(NSLOT)]
            self.ring16 = [r[:, :].bitcast(BF16) for r in self.ring32]
            self.ones = sb("ones", [P, P], BF16)
            self.gmix = sb("gmix", [P, self.nl * NCH], F32)
            self.gffn = sb("gffn", [P, self.nl * NCH], F32)
            self.gfin = sb("gfin", [P, NCH], F32)
            self.flags = sb("flags", [P, 2], F32)
            self.epsc = sb("epsc", [P, 1], F32)
            self.zeroc = sb("zeroc", [P, 1], F32)
            self.convw = sb("convw", [P, max(1, self.ne) * 24], F32)
            ARENA_BYTES = 62 * 1024
            arena_t = sb("arena", [P, ARENA_BYTES // 2], BF16)
            self.arena = Arena(arena_t, ARENA_BYTES)
            self.ps = [es.enter_context(nc.psum_tensor(f"ps{i}", [P, 512], F32)) for i in range(8)]
            self.psb = [Buf() for _ in range(8)]
            self.ringb = [Buf() for _ in range(NSLOT)]
            self.xb = [[Buf(), Buf()] for _ in range(NCH)]
            self.hb = [Buf() for _ in range(NCH)]
            self.constb = Buf()

            xv = self.d_x.rearrange("(c p) t -> p c t", p=P)
            for k in range(4):
                S.dma("sp", f"xin{k}", lambda e, k=k: e.dma_start(out=self.xT[:, 4 * k:4 * k + 4, :],
                                                                   in_=xv[:, 4 * k:4 * k + 4, :]),
                      writes=[b for c in range(4 * k, 4 * k + 4) for b in self.xb[c]])
            cl = [("cg0", self.gmix, self.d_gmix), ("cg1", self.gffn, self.d_gffn), ("cg2", self.gfin, self.d_gfin),
                  ("cg3", self.flags, self.d_flag)]
            if self.ne:
                cl.append(("cg4", self.convw, self.d_cw))
            ctoks = []
            for key, dst, src in cl:
                ctoks.append(S.dma("act", key, lambda e, o=dst, i=src: e.dma_start(out=o[:, :], in_=i[:, :])))
            ctoks.append(S.op("dve", lambda e: e.memset(self.ones[:, :], 1.0)))
            ctoks.append(S.op("dve", lambda e: e.memset(self.epsc[:, :], EPS)))
            ctoks.append(S.op("dve", lambda e: e.memset(self.zeroc[:, :], 0.0)))
            for t in ctoks:
                self.constb.w[t[0]] = max(self.constb.w.get(t[0], 0), t[1])

            outs = []
            ei = oi = 0
            import os
            dbg = os.environ.get("K_DBG", "")
            for li, l in enumerate(self.layers):
                if "nomix" not in dbg:
                    self.rmsnorm(self.gmix[:, li * NCH:(li + 1) * NCH])
                    if l % 2 == 0:
                        self.mixer_even(ei)
                        ei += 1
                    else:
                        self.mixer_odd(oi)
                        oi += 1
                if "noffn" not in dbg:
                    self.rmsnorm(self.gffn[:, li * NCH:(li + 1) * NCH])
                    self.ffn(li)
            if self.last:
                outs = self.rmsnorm(self.gfin, final=True)
            else:
                yv = self.d_out.rearrange("(c p) t -> p c t", p=P)
                for k in range(4):
                    outs.append(S.dma("sp", f"out{k}", lambda e, k=k: e.dma_start(
                        out=yv[:, 4 * k:4 * k + 4, :], in_=self.xT[:, 4 * k:4 * k + 4, :]),
                        reads=[b for c in range(4 * k, 4 * k + 4) for b in self.xb[c]]))
            S.wait_only("sp", outs)

            sems = {k: es.enter_context(nc.semaphore(f"s_{k}")) for k in sorted(S.keys)}
            block = es.enter_context(nc.Block())

            @block.tensor
            def _(e):
                S.replay("pe", e, sems)

            @block.scalar
            def _(e):
                S.replay("act", e, sems)

            @block.vector
            def _(e):
                S.replay("dve", e, sems)

            @block.gpsimd
            def _(e):
                S.replay("pool", e, sems)

            @block.sync
            def _(e):
                S.replay("sp", e, sems)
        return nc


def _bias_tables(rel_bias):
    k = np.arange(P)[:, None]
    q = np.arange(640)[None, :]
    idx = np.clip(q - k, -256, 256) + 256
    g = rel_bias[:, :, idx]
    ne = rel_bias.shape[0]
    g = g.reshape(ne, 4, 2, P, 640).transpose(0, 1, 3, 2, 4)
    return np.ascontiguousarray(g.reshape(ne * 4, P, 2 * 640)).astype(np.float32)


def _mask_neg():
    k = np.arange(P)[:, None] // 64
    q = np.arange(640)[None, :] // 64
    valid = (q - k >= 0) & (q - k <= 8)
    return np.where(valid, 0.0, NEG).astype(np.float32)


def _pm(v):
    v = np.asarray(v, dtype=np.float32).reshape(-1, NCH, P)
    return np.ascontiguousarray(v.transpose(2, 0, 1).reshape(P, -1))


_NC_CACHE = {}


def _run_segment(layers, first, last, xT_shards, inp):
    key = (tuple(layers), first, last)
    if key not in _NC_CACHE:
        planner = Builder(layers, first, last)
        planner.build()
        _NC_CACHE[key] = Builder(layers, first, last, plan=planner.slab_log).build()
    nc = _NC_CACHE[key]
    L = list(layers)
    ev = [l // 2 for l in L if l % 2 == 0]
    od = [l // 2 for l in L if l % 2 == 1]
    shared = {
        "g_mix": _pm(inp["mix_norm"][L]),
        "g_ffn": _pm(inp["ffn_norm"][L]),
        "g_fin": _pm(inp["final_norm"][None, :]),
    }
    import os
    if "noffn" not in os.environ.get("K_DBG", ""):
        shared["w_gate"] = np.ascontiguousarray(inp["ffn_w_gate"][L])
        shared["w_up"] = np.ascontiguousarray(inp["ffn_w_up"][L])
        shared["w_down"] = np.ascontiguousarray(inp["ffn_w_down"][L])
    if ev:
        shared["ab_w_in"] = np.ascontiguousarray(inp["ab_w_in"][ev])
        shared["ab_w_out"] = np.ascontiguousarray(inp["ab_w_out"][ev])
        shared["ab_biasg"] = _bias_tables(np.asarray(inp["ab_rel_bias"])[ev])
        shared["ab_maskneg"] = _mask_neg()
        cw = np.asarray(inp["ab_conv_w"])[ev].reshape(len(ev), 3, 8, P)
        shared["ab_convw"] = np.ascontiguousarray(cw.transpose(3, 0, 1, 2).reshape(P, len(ev) * 24))
    if od:
        shared["c_w_in"] = np.ascontiguousarray(inp["c_w_in"][od])
        shared["c_w_out"] = np.ascontiguousarray(inp["c_w_out"][od])
        shared["c_ln_g_rep"] = np.ascontiguousarray(np.broadcast_to(inp["c_ln_g"][od][:, None, :], (len(od), P, D)))
        shared["c_ln_b_rep"] = np.ascontiguousarray(np.broadcast_to(inp["c_ln_b"][od][:, None, :], (len(od), P, D)))
        shared["c_w_sT"] = np.ascontiguousarray(np.asarray(inp["c_w_s"])[od].transpose(0, 1, 3, 2))
        shared["c_b_s"] = np.ascontiguousarray(np.asarray(inp["c_b_s"])[od].reshape(len(od), 1, 1024))
    in_maps = []
    for c in range(N_CORES):
        m = dict(shared)
        m["xT_in"] = xT_shards[c]
        fl = np.zeros((P, 2), np.float32)
        if c % 2 == 1:
            fl[:, 0] = 1.0
            fl[:, 1] = 0.0
        else:
            fl[:, 0] = 0.0
            fl[:, 1] = NEG
        m["flags"] = fl
        in_maps.append(m)
    if _os.environ.get("K_TRACE"):
        res = run_bass_kernel_spmd(nc, in_maps, core_ids=list(range(N_CORES)), trace=True)
        print("EXEC_NS", res.exec_time_ns, flush=True)
    else:
        res = run_bass_kernel_spmd(nc, in_maps, core_ids=list(range(N_CORES)))
    return [np.asarray(r["yT_out"]) for r in res.results]


SEGMENTS = [[0, 1, 2, 3]]


def kernel(**inputs):
    inp = {k: np.asarray(v) for k, v in inputs.items()}
    x = inp["x"].astype(np.float32, copy=False)
    B, Sq, _ = x.shape
    shards = []
    for c in range(N_CORES):
        b, hf = c // 2, c % 2
        shards.append(np.ascontiguousarray(x[b, hf * T:(hf + 1) * T, :].T))
    for si, seg in enumerate(SEGMENTS):
        shards = _run_segment(seg, si == 0, si == len(SEGMENTS) - 1, shards, inp)
    out = np.empty((B, Sq, D), np.float32)
    for c in range(N_CORES):
        b, hf = c // 2, c % 2
        out[b, hf * T:(hf + 1) * T, :] = shards[c].T
    return out
```
